# Optimizing a Trainium2 kernel written in Bass

```python
import math
import jax
import jax.numpy as jnp
from jax import lax
import numpy as np

D_MODEL = 2048
BATCH = 1
SEQ = 8192
DEPTH = 1
DEC_BATCH = 32
DEC_SEQ = 4
PAST_LEN = 16384
PAGE_SIZE = 128

D_MIX = D_MODEL
D_ATTN = D_MIX // 2
D_CHUNK = D_MIX - D_ATTN
HEAD_DIM = 64
N_HEADS_A = D_ATTN // HEAD_DIM
CHUNK = 128
GROUP_WIDTH_B = 128
N_GROUPS_B = D_CHUNK // GROUP_WIDTH_B
DILATIONS = ((128, 1), (512, 4), (2048, 16))
MAX_WINDOW = 2048
BLOCK = 128
ROPE_THETA = 10000.0
EPS = 1e-6
NEG_INF = -1e30
SPLITS = (D_ATTN, 2 * D_ATTN, 3 * D_ATTN, 4 * D_ATTN, 4 * D_ATTN + D_CHUNK, 4 * D_ATTN + 2 * D_CHUNK)
D_IN = 4 * D_ATTN + 3 * D_CHUNK

kernel_name = 'hybrid_dilated_attn_chunk_gmlp_step'


def _rms_norm(x, g):
    xf = x.astype(jnp.float32)
    y = xf * lax.rsqrt(jnp.mean(xf * xf, axis=-1, keepdims=True) + EPS)
    return (y * g.astype(jnp.float32)).astype(x.dtype)


def _layer_norm(x, g, b):
    xf = x.astype(jnp.float32)
    mu = jnp.mean(xf, axis=-1, keepdims=True)
    var = jnp.mean(jnp.square(xf - mu), axis=-1, keepdims=True)
    y = (xf - mu) * lax.rsqrt(var + EPS)
    return (y * g.astype(jnp.float32) + b.astype(jnp.float32)).astype(x.dtype)


def _rope(x, pos):
    half = HEAD_DIM // 2
    inv = jnp.exp(-math.log(ROPE_THETA) * jnp.arange(half, dtype=jnp.float32) / half)
    ang = pos.astype(jnp.float32)[:, None] * inv[None, :]
    cos = jnp.cos(ang)[None, :, None, :]
    sin = jnp.sin(ang)[None, :, None, :]
    xf = x.astype(jnp.float32)
    x1, x2 = xf[..., :half], xf[..., half:]
    return jnp.concatenate([x1 * cos - x2 * sin, x2 * cos + x1 * sin], axis=-1).astype(x.dtype)


def _branch_inputs(x, pos, norm_g, w_in, ln_g, ln_b):
    B, T, _ = x.shape
    z = _rms_norm(x, norm_g) @ w_in
    q, k, v, ga, u, vc, gb = jnp.split(z, SPLITS, axis=-1)
    hs = (B, T, N_HEADS_A, HEAD_DIM)
    q = _rope(q.reshape(hs), pos)
    k = _rope(k.reshape(hs), pos)
    v = v.reshape(hs)
    u = jax.nn.gelu(u)
    vc = _layer_norm(jax.nn.gelu(vc), ln_g, ln_b)
    return q, k, v, ga, u, vc, gb


def _softmax_lse(s, mask):
    s = jnp.where(mask, s, NEG_INF)
    m = jnp.max(s, axis=-1, keepdims=True)
    p = jnp.exp(s - m)
    den = jnp.sum(p, axis=-1, keepdims=True)
    return p / den, (m + jnp.log(den))[..., 0]


def _combine(outs, lses):
    w = jax.nn.softmax(jnp.stack(lses, axis=0), axis=0)
    return jnp.sum(w[..., None] * jnp.stack(outs, axis=0), axis=0)


def _dilated_attention_prompt(q, k, v):
    B, S, H, hd = q.shape
    scale = HEAD_DIM ** -0.5
    qi = jnp.arange(BLOCK)[:, None]
    si = jnp.arange(2 * BLOCK)[None, :]
    dist = qi + BLOCK - si
    outs, lses = [], []
    for window, d in DILATIONS:
        span = window // d
        L = S // d
        nb = -(-L // BLOCK)
        Lp = nb * BLOCK

        def to_res(a):
            a = a.reshape(B, L, d, H, hd).transpose(0, 2, 1, 3, 4)
            a = jnp.pad(a, ((0, 0), (0, 0), (0, Lp - L), (0, 0), (0, 0)))
            return a.reshape(B, d, nb, BLOCK, H, hd).astype(jnp.float32)

        qb, kb, vb = to_res(q), to_res(k), to_res(v)
        pad_prev = ((0, 0), (0, 0), (1, 0), (0, 0), (0, 0), (0, 0))
        kcat = jnp.concatenate([jnp.pad(kb, pad_prev)[:, :, :nb], kb], axis=3)
        vcat = jnp.concatenate([jnp.pad(vb, pad_prev)[:, :, :nb], vb], axis=3)
        s = jnp.einsum('brnqhk,brnshk->brnhqs', qb, kcat) * scale
        band = (dist >= 0) & (dist <= span)
        has_prev = (jnp.arange(nb) > 0)[:, None, None] | (si >= BLOCK)[None]
        mask = (band[None] & has_prev)[None, None, :, None]
        p, lse = _softmax_lse(s, mask)
        o = jnp.einsum('brnhqs,brnshk->brnqhk', p, vcat)
        o = o.reshape(B, d, Lp, H, hd)[:, :, :L].transpose(0, 2, 1, 3, 4).reshape(B, S, H, hd)
        lse = lse.transpose(0, 1, 2, 4, 3).reshape(B, d, Lp, H)[:, :, :L].transpose(0, 2, 1, 3).reshape(B, S, H)
        outs.append(o)
        lses.append(lse)
    return _combine(outs, lses)


def _dilated_attention_sample(q, k_all, v_all, wb):
    T = q.shape[1]
    scale = HEAD_DIM ** -0.5
    qf = q.astype(jnp.float32)
    outs, lses = [], []
    for window, d in DILATIONS:
        J = window // d + 1
        idx = wb + jnp.arange(T)[:, None] - jnp.arange(J)[None, :] * d
        valid = idx >= 0
        idxc = jnp.maximum(idx, 0)
        kg = k_all[:, idxc].astype(jnp.float32)
        vg = v_all[:, idxc].astype(jnp.float32)
        s = jnp.einsum('bthk,btjhk->bthj', qf, kg) * scale
        p, lse = _softmax_lse(s, valid[None, :, None, :])
        outs.append(jnp.einsum('bthj,btjhk->bthk', p, vg))
        lses.append(lse)
    return _combine(outs, lses)


def _masked_spatial(w_s):
    return w_s * jnp.tril(jnp.ones((CHUNK, CHUNK), w_s.dtype))[None]


def _chunk_mlp_prompt(u, vc, w_s, b_s):
    B, S, _ = u.shape
    vr = vc.reshape(B, S // CHUNK, CHUNK, N_GROUPS_B, GROUP_WIDTH_B)
    mixed = jnp.einsum('gts,bcsgk->bctgk', _masked_spatial(w_s), vr) + jnp.transpose(b_s)[:, :, None]
    return u * mixed.reshape(B, S, D_CHUNK)


def _chunk_mlp_sample(u, vc, w_s, b_s):
    B, T, _ = u.shape
    vr = vc.reshape(B, T, N_GROUPS_B, GROUP_WIDTH_B)
    wm = _masked_spatial(w_s)[:, :T, :T]
    mixed = jnp.einsum('gts,bsgk->btgk', wm, vr) + jnp.transpose(b_s[:, :T])[:, :, None]
    return u * mixed.reshape(B, T, D_CHUNK)


def _output(o_attn, ga, o_chunk, gb, x, w_out):
    B, T, _ = x.shape
    a = o_attn.reshape(B, T, D_ATTN).astype(x.dtype) * jax.nn.silu(ga)
    c = o_chunk * jax.nn.silu(gb)
    return x + jnp.concatenate([a, c], axis=-1) @ w_out


def setup_inputs(seed: int = 0) -> dict:
    key = jax.random.key(seed)
    ks = jax.random.split(key, 14)
    wb = min(MAX_WINDOW, PAST_LEN)
    nrm = jax.random.normal
    f32 = jnp.float32
    return {
        'x_prompt': nrm(ks[0], (BATCH, SEQ, D_MODEL), f32),
        'x_sample': nrm(ks[1], (DEC_BATCH, DEC_SEQ, D_MODEL), f32),
        'cache_k': nrm(ks[2], (DEPTH, DEC_BATCH, wb, N_HEADS_A, HEAD_DIM), f32),
        'cache_v': nrm(ks[3], (DEPTH, DEC_BATCH, wb, N_HEADS_A, HEAD_DIM), f32),
        'norm_g': 1.0 + 0.02 * nrm(ks[4], (DEPTH, D_MODEL), f32),
        'w_in': nrm(ks[5], (DEPTH, D_MODEL, D_IN), f32) * D_MODEL ** -0.5,
        'ln_g': 1.0 + 0.02 * nrm(ks[6], (DEPTH, D_CHUNK), f32),
        'ln_b': 0.02 * nrm(ks[7], (DEPTH, D_CHUNK), f32),
        'w_s': nrm(ks[8], (DEPTH, N_GROUPS_B, CHUNK, CHUNK), f32) * CHUNK ** -0.5,
        'b_s': 1.0 + 0.02 * nrm(ks[9], (DEPTH, N_GROUPS_B, CHUNK), f32),
        'w_out': nrm(ks[10], (DEPTH, D_MIX, D_MODEL), f32) * D_MIX ** -0.5,
        'final_g': 1.0 + 0.02 * nrm(ks[11], (D_MODEL,), f32),
    }


def reference(x_prompt, x_sample, cache_k, cache_v, norm_g, w_in, ln_g, ln_b, w_s, b_s, w_out, final_g):
    S = x_prompt.shape[1]
    T = x_sample.shape[1]
    wb = cache_k.shape[2]
    tail = min(MAX_WINDOW, S)
    pos_p = jnp.arange(S, dtype=jnp.int32)
    pos_s = PAST_LEN + jnp.arange(T, dtype=jnp.int32)
    xp, xs = x_prompt, x_sample
    kp_rows, vp_rows, ks_rows, vs_rows, cs_rows = [], [], [], [], []
    for l in range(DEPTH):
        q, k, v, ga, u, vc, gb = _branch_inputs(xp, pos_p, norm_g[l], w_in[l], ln_g[l], ln_b[l])
        oa = _dilated_attention_prompt(q, k, v)
        oc = _chunk_mlp_prompt(u, vc, w_s[l], b_s[l])
        xp = _output(oa, ga, oc, gb, xp, w_out[l])
        kp_rows.append(k[:, S - tail:])
        vp_rows.append(v[:, S - tail:])
        q, k, v, ga, u, vc, gb = _branch_inputs(xs, pos_s, norm_g[l], w_in[l], ln_g[l], ln_b[l])
        k_all = jnp.concatenate([cache_k[l].astype(k.dtype), k], axis=1)
        v_all = jnp.concatenate([cache_v[l].astype(v.dtype), v], axis=1)
        oa = _dilated_attention_sample(q, k_all, v_all, wb)
        oc = _chunk_mlp_sample(u, vc, w_s[l], b_s[l])
        xs = _output(oa, ga, oc, gb, xs, w_out[l])
        ks_rows.append(k)
        vs_rows.append(v)
        cs_rows.append(vc)
    y_prompt = _rms_norm(xp, final_g)
    y_sample = _rms_norm(xs, final_g)
    return (y_prompt, y_sample, jnp.stack(kp_rows), jnp.stack(vp_rows), jnp.stack(ks_rows), jnp.stack(vs_rows), jnp.stack(cs_rows))
```

```python
import numpy as np
import concourse.bass as bass
import concourse.mybir as mybir
from concourse.bass_utils import run_bass_kernel_spmd

F32, BF16 = mybir.dt.float32, mybir.dt.bfloat16
AF = mybir.ActivationFunctionType
ALU = mybir.AluOpType

NCORES = 8
TRUNC = None
D = 2048
DIN = 7168
TOWN = 1024
THALO = 2048
EPS = 1e-6
SCALE = 0.125
C_Q, C_K, C_V, C_GA, C_U, C_VC, C_GB = 0, 1024, 2048, 3072, 4096, 5120, 6144


class Sched:
    def __init__(self):
        self.ops = []
        self.last_w = {}
        self.readers = {}
        self.bar = None
        self.seen_bar = {}
        self.marks = {}

    def add(self, eng, fn, r=(), w=(), dma=False):
        deps = set()
        for k in r:
            if k in self.last_w:
                deps.add(self.last_w[k])
        for k in w:
            if k in self.last_w:
                deps.add(self.last_w[k])
            deps.update(self.readers.get(k, ()))
        if self.bar is not None and not self.seen_bar.get(eng):
            deps.update(self.bar)
            self.seen_bar[eng] = True
        i = len(self.ops)
        self.ops.append(dict(eng=eng, fn=fn, deps=deps, dma=dma))
        for k in r:
            self.readers.setdefault(k, []).append(i)
        for k in w:
            self.last_w[k] = i
            self.readers[k] = []
        return i

    def barrier(self):
        last = {}
        dmas = []
        for i, o in enumerate(self.ops):
            if o['dma']:
                dmas.append(i)
            else:
                last[o['eng']] = i
        self.bar = set(last.values()) | set(dmas[-64:])
        self.seen_bar = {}

    def emit(self, nc, block, sems, dma_sems):
        ops = self.ops
        NS = len(next(iter(dma_sems.values())))
        need_sig = [False] * len(ops)
        for i, o in enumerate(ops):
            for d in o['deps']:
                po = ops[d]
                if po['dma']:
                    continue
                if po['eng'] == o['eng'] and o['eng'] == 'pe' and not o['dma']:
                    continue
                need_sig[d] = True
        sigval = [None] * len(ops)
        cnt = {e: 0 for e in sems}
        dcnt = {e: 0 for e in dma_sems}
        for i, o in enumerate(ops):
            e = o['eng']
            if o['dma']:
                k = dcnt[e]
                dcnt[e] += 1
                sigval[i] = ('d', e, k % NS, 16 * (k // NS + 1))
            elif need_sig[i]:
                cnt[e] += 1
                sigval[i] = ('c', e, None, cnt[e])
        final_dma = {}
        for i, o in enumerate(ops):
            if o['dma']:
                _, e, s, v = sigval[i]
                final_dma[(e, s)] = v

        def run(eng_name, E):
            waited = {}

            def wait(key, sem, val):
                if waited.get(key, 0) < val:
                    E.wait_ge(sem, val)
                    waited[key] = val

            for i, o in enumerate(ops):
                if o['eng'] != eng_name:
                    continue
                for d in sorted(o['deps']):
                    sv = sigval[d]
                    if sv is None:
                        continue
                    kind, pe_, s, v = sv
                    if kind == 'd':
                        wait(('d', pe_, s), dma_sems[pe_][s], v)
                    else:
                        if pe_ == eng_name and eng_name == 'pe' and not o['dma']:
                            continue
                        wait(('c', pe_), sems[pe_], v)
                if o['dma']:
                    _, e, s, v = sigval[i]
                    if v > 16:
                        wait(('d', e, s), dma_sems[e][s], v - 16)
                    ins = o['fn'](E)
                    ins.then_inc(dma_sems[e][s], 16)
                else:
                    ins = o['fn'](E)
                    if sigval[i] is not None:
                        ins.then_inc(sems[eng_name], 1)
            if eng_name in dma_sems:
                for (e, s), v in final_dma.items():
                    if e == eng_name:
                        wait(('d', e, s), dma_sems[e][s], v)

        @block.tensor
        def _(E):
            run('pe', E)

        @block.scalar
        def _(E):
            run('act', E)

        @block.vector
        def _(E):
            run('dve', E)

        @block.gpsimd
        def _(E):
            run('pool', E)

        @block.sync
        def _(E):
            run('sp', E)


def sl(start, n, step):
    return slice(start, start + (n - 1) * step + 1, step)


class Arena:
    def __init__(self, ap, nwords):
        self.ap = ap
        self.n = nwords
        self.p = 0

    def f32(self, n):
        self.p = (self.p + 15) // 16 * 16
        a = self.ap[:, self.p:self.p + n]
        self.p += n
        assert self.p <= self.n, ("arena overflow", self.p, self.n)
        return a

    def bf(self, n):
        assert n % 2 == 0
        return self.f32(n // 2).bitcast(BF16)


def build_program():
    nc = bass.Bass("TRN2", target_bir_lowering=False)

    def din(name, shape, dt=F32):
        return nc.dram_tensor(name, list(shape), dt, kind="ExternalInput").ap()

    def dout(name, shape, dt=F32):
        return nc.dram_tensor(name, list(shape), dt, kind="ExternalOutput").ap()

    x_own = din("x_own", [TOWN, D])
    x_halo = din("x_halo", [THALO, D])
    x_samp = din("x_samp", [16, D])
    ck = din("ck", [4, 2048, 1024])
    cv = din("cv", [4, 2048, 1024])
    w_in = din("w_in", [D, DIN])
    w_out = din("w_out", [D, D])
    norm_g = din("norm_g", [D])
    final_g = din("final_g", [D])
    ln_g = din("ln_g", [1024])
    ln_b = din("ln_b", [1024])
    b_s = din("b_s", [1024])
    wsT = din("wsT", [128, 8 * 128])
    wsS = din("wsS", [16, 8 * 16])
    tab_d = din("tab", [128, 25 * 128])
    ident_d = din("ident", [128, 128])
    ones_d = din("ones", [128, 128])
    maskT_d = din("maskT", [128, 256])
    masks_d = din("masks", [128, 4 * 10 * 8])
    validp_d = din("validp", [128, 53 * 64])
    zeros_d = din("zeros", [128, 4096])

    y_own = dout("y_own", [TOWN, D])
    y_samp = dout("y_samp", [16, D])
    k_own = dout("k_own", [TOWN, 1024])
    v_own = dout("v_own", [TOWN, 1024])
    k_samp = dout("k_samp", [16, 1024])
    v_samp = dout("v_samp", [16, 1024])
    vc_samp = dout("vc_samp", [16, 1024])

    vscr = nc.dram_tensor("vscr", [3072, 1024], BF16, kind="Internal").ap()

    w_in_v = w_in.rearrange("(j p) c -> p j c", p=128)
    w_out_v = w_out.rearrange("(j p) c -> p j c", p=128)

    S = Sched()
    NW = 53200

    from contextlib import ExitStack
    with ExitStack() as es:
        arena_t = es.enter_context(nc.sbuf_tensor("arena", [128, NW], F32))
        ps = es.enter_context(nc.psum_tensor("ps", [128, 4096], F32))
        sems = {e: es.enter_context(nc.semaphore("s_" + e)) for e in ('pe', 'act', 'dve', 'pool')}
        NS = 8
        dma_sems = {e: [es.enter_context(nc.semaphore("d_%s%d" % (e, i))) for i in range(NS)]
                    for e in ('sp', 'pool', 'act')}
        block = es.enter_context(nc.Block())

        A = Arena(arena_t, NW)
        psb = ps[:, :].bitcast(BF16)

        ident = A.bf(128)
        ones = A.bf(128)
        maskT = A.bf(256).rearrange("p (c q) -> p c q", c=2)
        tab = A.f32(25 * 128).rearrange("p (b f) -> p b f", b=25)
        sm = A.f32(64)
        QTA = A.bf(8 * 1040).rearrange("p (h t) -> p h t", h=8)
        base_mark = A.p

        S.add('pool', lambda E: E.dma_start(out=ident, in_=ident_d), w=['ident'], dma=True)
        S.add('pool', lambda E: E.dma_start(out=ones, in_=ones_d), w=['ones'], dma=True)
        S.add('pool', lambda E: E.dma_start(out=maskT.rearrange("p c q -> p (c q)"), in_=maskT_d), w=['maskT'], dma=True)

        kt_off = (A.p + 15) // 16 * 16
        KT = A.bf(8 * 3072).rearrange("p (h t) -> p h t", h=8)
        KTs = A.bf(8 * 16).rearrange("p (h t) -> p h t", h=8)
        Vs = A.bf(1024)
        KText = arena_t[:, kt_off:A.p].bitcast(BF16)
        ph_mark = A.p
        B = dict(xnT=A.bf(16 * 1040).rearrange("p (j t) -> p j t", j=16),
                 Wst=[A.bf(16 * 512).rearrange("p (j c) -> p j c", j=16) for _ in range(2)],
                 xst=[A.f32(2048) for _ in range(2)],
                 xn=[A.bf(2048) for _ in range(3)],
                 gbc=A.f32(2048))
        raw = [A.f32(512) for _ in range(2)]
        rt = [A.f32(512) for _ in range(2)]
        rtmp = [A.f32(512) for _ in range(2)]
        rof = [A.f32(512) for _ in range(2)]
        rob = [A.bf(512) for _ in range(3)]
        vbf = [A.bf(512) for _ in range(4)]
        vf = [A.f32(512) for _ in range(2)]

        S.add('sp', lambda E, g=B['gbc']: E.dma_start(out=g, in_=norm_g.partition_broadcast(128)), w=['gbc'], dma=True)

        cnt = dict(x=0, w=0, nat=0, tr=0, raw=0, ro=0, v=0, kt=0)
        deferred = []

        def flush_deferred(keep=0):
            while len(deferred) > keep:
                deferred.pop(0)()

        build_q = []

        def build_block(src, NT, tok0, defer=0):
            xnT, xst, xn, gbc = B['xnT'], B['xst'], B['xn'], B['gbc']
            s = cnt['x'] % 2
            n3 = cnt['x'] % 3
            cnt['x'] += 1
            xs = xst[s]
            xnb = xn[n3]
            c0 = 32 + 4 * n3
            bk = 'xnT_b%d' % (tok0 // 128)
            S.add('sp', lambda E: E.dma_start(out=xs[0:NT, :], in_=src), w=['xst%d' % s], dma=True)

            def stage1():
                S.add('act', lambda E: E.activation(out=xnb[0:NT, :], in_=xs[0:NT, :], func=AF.Square, accum_out=sm[0:NT, c0:c0 + 1]),
                      r=['xst%d' % s], w=['xn%d' % n3, 'ssq%d' % n3])
                S.add('dve', lambda E: E.tensor_scalar(out=sm[0:NT, c0 + 1:c0 + 2], in0=sm[0:NT, c0:c0 + 1], scalar1=1.0 / D, scalar2=EPS, op0=ALU.mult, op1=ALU.add),
                      r=['ssq%d' % n3], w=['ms%d' % n3])
                S.add('act', lambda E: E.activation(out=sm[0:NT, c0 + 2:c0 + 3], in_=sm[0:NT, c0 + 1:c0 + 2], func=AF.Sqrt), r=['ms%d' % n3], w=['sd%d' % n3])
                S.add('dve', lambda E: E.reciprocal(out=sm[0:NT, c0 + 3:c0 + 4], in_=sm[0:NT, c0 + 2:c0 + 3]), r=['sd%d' % n3], w=['rstd%d' % n3])
                S.add('dve', lambda E: E.scalar_tensor_tensor(out=xnb[0:NT, :], in0=xs[0:NT, :], scalar=sm[0:NT, c0 + 3:c0 + 4], in1=gbc[0:NT, :],
                                                              op0=ALU.mult, op1=ALU.mult),
                      r=['xst%d' % s, 'rstd%d' % n3, 'gbc'], w=['xn%d' % n3])
            if defer != 'manual3':
                stage1()

            def stage2():
                tp = cnt['tr'] % 2
                cnt['tr'] += 1
                pst = psb[:, tp * 2048:(tp + 1) * 2048].rearrange("p (j t) -> p j t", j=16)

                def tr(E):
                    ins = None
                    for j in range(16):
                        ins = E.transpose(out=pst[:, j, 0:NT], in_=xnb[0:NT, j * 128:(j + 1) * 128], identity=ident[0:NT, 0:NT])
                    return ins
                S.add('pe', tr, r=['xn%d' % n3, 'ident'], w=['pst%da' % tp, 'pst%db' % tp])
                S.add('act', lambda E: E.activation(out=xnT[:, 0:8, tok0:tok0 + NT], in_=pst[:, 0:8, 0:NT], func=AF.Copy), r=['pst%da' % tp], w=[bk + 'a'])
                S.add('dve', lambda E: E.tensor_copy(out=xnT[:, 8:16, tok0:tok0 + NT], in_=pst[:, 8:16, 0:NT]), r=['pst%db' % tp], w=[bk + 'b'])
            if defer == 'manual':
                return stage2
            if defer == 'manual3':
                return stage1, stage2
            if defer:
                build_q.append(stage2)
                while len(build_q) > defer:
                    build_q.pop(0)()
            else:
                stage2()

        def flush_builds():
            while build_q:
                build_q.pop(0)()

        def load_w(c0, eng='pool'):
            s = cnt['w'] % 2
            cnt['w'] += 1
            Wst = B['Wst']
            S.add(eng, lambda E, s=s, c0=c0: E.dma_start(out=Wst[s][:, :, 0:256], in_=w_in_v[:, :, c0:c0 + 256]), w=['W%da' % s], dma=True)
            S.add(eng, lambda E, s=s, c0=c0: E.dma_start(out=Wst[s][:, :, 256:512], in_=w_in_v[:, :, c0 + 256:c0 + 512]), w=['W%db' % s], dma=True)
            return s

        def nat_matmul(ws, NT, tok0):
            s = cnt['nat'] % 2
            cnt['nat'] += 1
            xnT, Wst = B['xnT'], B['Wst']
            bank = ps[:, (4 + s) * 512:(5 + s) * 512]
            bk = 'xnT_b%d' % (tok0 // 128)

            def mm(E, ws=ws, NT=NT, tok0=tok0, bank=bank):
                ins = None
                for j in range(16):
                    ins = E.matmul(out=bank[0:NT, :], lhsT=xnT[:, j, tok0:tok0 + NT], rhs=Wst[ws][:, j, :], start=(j == 0), stop=(j == 15))
                return ins
            S.add('pe', mm, r=[bk + 'a', bk + 'b', 'W%da' % ws, 'W%db' % ws], w=['nat%d' % s])
            return bank, 'nat%d' % s

        def rope_bank(bank, bkey, NT, tblk, out_f32_dram=None):
            s = cnt['raw'] % 2
            cnt['raw'] += 1
            rw = raw[s]
            rts, rtm = rt[s], rtmp[s]
            S.add('act', lambda E: E.activation(out=rw[0:NT, :], in_=bank[0:NT, :], func=AF.Copy), r=[bkey], w=['raw%d' % s])
            rw4 = rw.rearrange("p (h two f) -> p h two f", h=8, two=2)
            rt3 = rts.rearrange("p (h f) -> p h f", h=8)
            tm4 = rtm.rearrange("p (h two f) -> p h two f", h=8, two=2)
            Cb = tab[0:NT, tblk, 0:64].unsqueeze(1).to_broadcast([NT, 8, 64])
            SNb = tab[0:NT, tblk, 64:96].unsqueeze(1).to_broadcast([NT, 8, 32])
            SPb = tab[0:NT, tblk, 96:128].unsqueeze(1).to_broadcast([NT, 8, 32])
            S.add('pool', lambda E: E.tensor_tensor(out=rt3[0:NT], in0=rw.rearrange("p (h f) -> p h f", h=8)[0:NT], in1=Cb, op=ALU.mult),
                  r=['raw%d' % s, 'tab'], w=['rt%d' % s])
            S.add('dve', lambda E: E.tensor_tensor(out=tm4[0:NT, :, 0, :], in0=rw4[0:NT, :, 1, :], in1=SNb, op=ALU.mult),
                  r=['raw%d' % s, 'tab'], w=['rtmpA%d' % s])
            S.add('dve', lambda E: E.tensor_tensor(out=tm4[0:NT, :, 1, :], in0=rw4[0:NT, :, 0, :], in1=SPb, op=ALU.mult),
                  r=['raw%d' % s, 'tab'], w=['rtmpB%d' % s])
            o = cnt['ro'] % 2
            ob = cnt['ro'] % 3
            cnt['ro'] += 1
            S.add('dve', lambda E: E.tensor_tensor(out=rob[ob][0:NT, :], in0=rts[0:NT, :], in1=rtm[0:NT, :], op=ALU.add),
                  r=['rt%d' % s, 'rtmpA%d' % s, 'rtmpB%d' % s], w=['rob%d' % ob])
            if out_f32_dram is not None:
                S.add('pool', lambda E: E.tensor_tensor(out=rof[o][0:NT, :], in0=rts[0:NT, :], in1=rtm[0:NT, :], op=ALU.add),
                      r=['rt%d' % s, 'rtmpA%d' % s, 'rtmpB%d' % s], w=['rof%d' % o])
                S.add('sp', lambda E: E.dma_start(out=out_f32_dram, in_=rof[o][0:NT, :]), r=['rof%d' % o], dma=True)
            return rob[ob], 'rob%d' % ob

        def transpose_to(robt, rkey, NT, dst_fn, dst_key):
            s = cnt['kt'] % 2
            cnt['kt'] += 1
            pk = psb[:, (6 + s) * 1024:(6 + s) * 1024 + 512].rearrange("p (i t) -> p i t", i=4)

            def emit():
                def tr(E):
                    ins = None
                    for i in range(4):
                        ins = E.transpose(out=pk[:, i, 0:NT], in_=robt[0:NT, i * 128:(i + 1) * 128], identity=ident[0:NT, 0:NT])
                    return ins
                S.add('pe', tr, r=[rkey, 'ident'], w=['pk%d' % s])
                S.add('act', lambda E: E.activation(out=dst_fn(), in_=pk[:, :, 0:NT], func=AF.Copy), r=['pk%d' % s], w=[dst_key])
            deferred.append(emit)

        def v_bank(bank, bkey, NT, vrow0, out_f32_dram, c0, samp_dst=None):
            s = cnt['v'] % 4
            cnt['v'] += 1
            if samp_dst is None:
                S.add('act', lambda E: E.activation(out=vbf[s][0:NT, :], in_=bank[0:NT, :], func=AF.Copy), r=[bkey], w=['vbf%d' % s])
                S.add('sp', lambda E: E.dma_start(out=vscr[vrow0:vrow0 + NT, c0:c0 + 512], in_=vbf[s][0:NT, :]), r=['vbf%d' % s], w=['vscr'], dma=True)
            else:
                S.add('act', lambda E: E.activation(out=samp_dst, in_=bank[0:NT, :], func=AF.Copy), r=[bkey], w=['vsamp'])
            if out_f32_dram is not None:
                s3 = cnt['v'] % 2
                S.add('act', lambda E: E.activation(out=vf[s3][0:NT, :], in_=bank[0:NT, :], func=AF.Copy), r=[bkey], w=['vf%d' % s3])
                S.add('sp', lambda E: E.dma_start(out=out_f32_dram, in_=vf[s3][0:NT, :]), r=['vf%d' % s3], dma=True)

        own_blocks = [(x_own[b * 128:(b + 1) * 128, :], 128, b * 128) for b in range(8)] + [(x_samp[:, :], 16, 1024)]
        halo_blocks = [[(x_halo[hg * 1024 + b * 128: hg * 1024 + (b + 1) * 128, :], 128, b * 128) for b in range(8)] for hg in range(2)]

        def evac_halo(hg):
            def f(cg, b, NT, tok0, bank, bkey):
                ltok = hg * 1024 + b * 128
                if cg >= 2:
                    kc = cg - 2
                    robt, rkey = rope_bank(bank, bkey, 128, hg * 8 + b)
                    transpose_to(robt, rkey, 128, lambda: KT[:, kc * 4:(kc + 1) * 4, ltok:ltok + 128], 'KT')
                else:
                    v_bank(bank, bkey, 128, ltok, None, (cg % 2) * 512)
            return f

        def evac_own(cg, b, NT, tok0, bank, bkey):
            kind = cg // 2
            tblk = 16 + b
            h4 = (cg % 2) * 4
            cc = (cg % 2) * 512
            if kind == 0:
                robt, rkey = rope_bank(bank, bkey, NT, tblk)
                transpose_to(robt, rkey, NT, lambda: QTA[:, h4:h4 + 4, tok0:tok0 + NT], 'QT')
            elif kind == 1:
                if b < 8:
                    robt, rkey = rope_bank(bank, bkey, NT, tblk, out_f32_dram=k_own[tok0:tok0 + 128, cc:cc + 512])
                    transpose_to(robt, rkey, NT, lambda: KT[:, h4:h4 + 4, 2048 + tok0:2048 + tok0 + 128], 'KT')
                else:
                    robt, rkey = rope_bank(bank, bkey, NT, tblk, out_f32_dram=k_samp[:, cc:cc + 512])
                    transpose_to(robt, rkey, NT, lambda: KTs[:, h4:h4 + 4, 0:16], 'KTs')
            else:
                if b < 8:
                    v_bank(bank, bkey, NT, 2048 + tok0, v_own[tok0:tok0 + 128, cc:cc + 512], cc)
                else:
                    v_bank(bank, bkey, NT, 0, v_samp[:, cc:cc + 512], cc, samp_dst=Vs[0:16, cc:cc + 512])

        groups = [
            dict(blocks=halo_blocks[0], cols=[C_V, C_V + 512, C_K, C_K + 512], evac=evac_halo(0)),
            dict(blocks=halo_blocks[1], cols=[C_V, C_V + 512, C_K, C_K + 512], evac=evac_halo(1)),
            dict(blocks=own_blocks, cols=[C_Q, C_Q + 512, C_K, C_K + 512, C_V, C_V + 512], evac=evac_own),
        ]
        allcg = [(gi, ci) for gi, g in enumerate(groups) for ci in range(len(g['cols']))]
        ws_next = load_w(groups[0]['cols'][0])
        g0 = groups[0]['blocks']
        g0s1, g0s2 = {}, {}
        g0n0, g0n1 = [0], [0]

        def g0_advance(k):
            while g0n0[0] < min(k + 1, len(g0)):
                g0s1[g0n0[0]], g0s2[g0n0[0]] = build_block(*g0[g0n0[0]], defer='manual3')
                g0n0[0] += 1
            while g0n1[0] < min(k, len(g0)):
                g0s1.pop(g0n1[0])()
                g0n1[0] += 1
        g0_advance(1)
        g0_advance(2)
        g0s2.pop(0)()
        S.add('sp', lambda E: E.dma_start(out=tab.rearrange("p b f -> p (b f)"), in_=tab_d), w=['tab'], dma=True)
        LA = 1
        for idx, (gi, ci) in enumerate(allcg):
            g = groups[gi]
            ws = ws_next
            if idx + 1 < len(allcg) and idx > 0:
                g2, c2 = allcg[idx + 1]
                ws_next = load_w(groups[g2]['cols'][c2])
            last_cg = (ci == len(g['cols']) - 1)
            nxt = groups[gi + 1]['blocks'] if (last_cg and gi + 1 < len(groups)) else []
            st1 = {}
            st2 = {}
            n0 = [0]
            n1 = [0]

            def stage1_upto(k):
                while n0[0] < min(k + 1, len(nxt)):
                    st1[n0[0]], st2[n0[0]] = build_block(*nxt[n0[0]], defer='manual3')
                    n0[0] += 1
                while n1[0] < min(k, len(nxt)):
                    st1.pop(n1[0])()
                    n1[0] += 1
            stage1_upto(LA)
            nb = len(g['blocks'])
            for b, (src, NT, tok0) in enumerate(g['blocks']):
                stage1_upto(b + 1 + LA)
                if idx == 0:
                    g0_advance(b + 3)
                    if b == 2:
                        ws_next = load_w(groups[allcg[1][0]]['cols'][allcg[1][1]])
                bank, bkey = nat_matmul(ws, NT, tok0)
                flush_deferred(keep=1)
                if idx == 0 and (b + 1) in g0s2:
                    g0s2.pop(b + 1)()
                if b - 1 in st2:
                    st2.pop(b - 1)()
                g['evac'](ci, b, NT, tok0, bank, bkey)
            stage1_upto(len(nxt))
            for k in sorted(st2):
                st2.pop(k)()
        flush_deferred()

        S.marks['O1'] = len(S.ops)
        hw_marks = {'H/O1': A.p}
        S.barrier()

        A.p = ph_mark
        def vtile(n):
            return A.bf(n * 256)
        Vd = [dict(d1=vtile(9).rearrange("p (m e c) -> p m e c", m=9, e=2),
                   d4=vtile(12).rearrange("p (m r e c) -> p m r e c", m=3, r=4, e=2),
                   d16a=vtile(16).rearrange("p (r e c) -> p r e c", r=16, e=2),
                   d16b=vtile(16).rearrange("p (r e c) -> p r e c", r=16, e=2)) for _ in range(2)]
        vp_all = A.bf(53 * 64)
        vp = dict(d1=vp_all[:, 0:9 * 64].rearrange("p (m c) -> p m c", m=9),
                  d4=vp_all[:, 9 * 64:21 * 64].rearrange("p (m r c) -> p m r c", m=3, r=4),
                  d16a=vp_all[:, 21 * 64:37 * 64].rearrange("p (r c) -> p r c", r=16),
                  d16b=vp_all[:, 37 * 64:53 * 64].rearrange("p (r c) -> p r c", r=16))
        PTf = [A.bf(512) for _ in range(3)]
        rD = A.f32(1024)
        XYc = A.f32(2048)
        Kc = [A.bf(9 * 128).rearrange("p (m c) -> p m c", m=9) for _ in range(2)]
        Vc = [A.bf(9 * 128).rearrange("p (m c) -> p m c", m=9) for _ in range(2)]
        KTc = [A.bf(9 * 128).rearrange("p (m c) -> p m c", m=9) for _ in range(2)]
        QsBD = A.bf(4 * 8 * 8).rearrange("p (b h q) -> p b h q", b=4, h=8)
        PTs = [A.bf(80).rearrange("p (m q) -> p m q", m=10) for _ in range(2)]
        masks = A.bf(320).rearrange("p (b m q) -> p b m q", b=4, m=10)
        rDs = [A.f32(16) for _ in range(2)]
        os_ = [A.f32(16) for _ in range(2)]
        QZ = A.bf(2 * 8 * 1024).rearrange("p (e h t) -> p e h t", e=2, h=8)

        S.add('pool', lambda E: E.dma_start(out=vp_all, in_=validp_d), w=['vp'], dma=True)
        S.add('pool', lambda E: E.dma_start(out=masks.rearrange("p b m q -> p (b m q)"), in_=masks_d), w=['masks'], dma=True)
        S.add('sp', lambda E: E.dma_start(out=QZ[64:128, 0].rearrange("p h t -> p (h t)").bitcast(F32), in_=zeros_d[0:64, :]), w=['QZ0b'], dma=True)
        S.add('sp', lambda E: E.dma_start(out=QZ[0:64, 1].rearrange("p h t -> p (h t)").bitcast(F32), in_=zeros_d[64:128, :]), w=['QZ1a'], dma=True)
        S.add('act', lambda E: E.activation(out=QZ[0:64, 0, 0:4], in_=QTA[0:64, 0:4, 0:1024], func=AF.Copy), r=['QT'], w=['QZ0a'])
        S.add('dve', lambda E: E.tensor_copy(out=QZ[0:64, 0, 4:8], in_=QTA[0:64, 4:8, 0:1024]), r=['QT'], w=['QZ0a2'])
        S.add('dve', lambda E: E.tensor_copy(out=QZ[64:128, 1, 0:4], in_=QTA[64:128, 0:4, 0:1024]), r=['QT'], w=['QZ1b'])
        S.add('act', lambda E: E.activation(out=QZ[64:128, 1, 4:8], in_=QTA[64:128, 4:8, 0:1024], func=AF.Copy), r=['QT'], w=['QZ1b2'])
        VKEYS = ('1', '40', '41', '42', '16a', '16b')
        for vs in range(2):
            S.add('pool', lambda E, vs=vs: E.memset(Vd[vs]['d16b'][64:128].rearrange("p r e c -> p (r e c)"), 0.0), w=['Vd%d_16b' % vs])
            for nm, key in (('d1', '1'), ('d16a', '16a'), ('d16b', '16b')):
                NP = 64 if nm == 'd16b' else 128
                S.add('dve', lambda E, vs=vs, nm=nm, NP=NP: E.tensor_copy(out=Vd[vs][nm][0:NP, :, 0, 64:128], in_=vp[nm][0:NP]), r=['vp'], w=['Vd%d_%s' % (vs, key)])
                S.add('act', lambda E, vs=vs, nm=nm, NP=NP: E.activation(out=Vd[vs][nm][0:NP, :, 1, 0:64], in_=vp[nm][0:NP], func=AF.Copy), r=['vp'], w=['Vd%d_%s' % (vs, key)])
            for m in range(3):
                S.add('dve', lambda E, vs=vs, m=m: E.tensor_copy(out=Vd[vs]['d4'][:, m, :, 0, 64:128], in_=vp['d4'][:, m]), r=['vp'], w=['Vd%d_4%d' % (vs, m)])
                S.add('act', lambda E, vs=vs, m=m: E.activation(out=Vd[vs]['d4'][:, m, :, 1, 0:64], in_=vp['d4'][:, m], func=AF.Copy), r=['vp'], w=['Vd%d_4%d' % (vs, m)])

        XY = [ps[:, 0:1024], ps[:, 1024:2048]]
        ucnt = [0]
        pending_pv = []

        def attn_unit(hp, vs, kt0, kt1, nk1, qsl, NQ, V0, V1, pvlist, vkeys):
            s = ucnt[0] % 3
            ucnt[0] += 1
            pS = ps[:, (4 + s) * 512:(4 + s) * 512 + 4 * NQ].rearrange("p (c e q) -> p c e q", c=2, e=2)
            PTv = PTf[s][:, 0:4 * NQ].rearrange("p (c e q) -> p c e q", c=2, e=2)

            def qk(E):
                E.matmul(out=pS[:, 0, :, 0:NQ], lhsT=KT[:, hp, kt0], rhs=QZ[:, :, hp, qsl], start=True, stop=True)
                if NQ == 64 and hp < 7:
                    k1 = KText[:, sl(hp * 3072 + kt1.start, 128, 16)]
                else:
                    k1 = KT[:, hp, kt1]
                nrow = 64 if (NQ == 64 and hp == 7) else nk1
                return E.matmul(out=pS[0:nrow, 1, :, 0:NQ], lhsT=k1, rhs=QZ[:, :, hp, qsl], start=True, stop=True)
            S.add('pe', qk, r=['KT', 'QZ0a', 'QZ0a2', 'QZ0b', 'QZ1a', 'QZ1b', 'QZ1b2'], w=['pS%d' % s])
            S.add('act', lambda E: E.activation(out=PTv[:, :, :, 0:NQ], in_=pS[:, :, :, 0:NQ], func=AF.Exp, scale=SCALE),
                  r=['pS%d' % s], w=['PT%d' % s])
            mk = maskT[:, :, 0:NQ].unsqueeze(2).to_broadcast([128, 2, 2, NQ])
            S.add('dve', lambda E: E.tensor_tensor(out=PTv[:, :, :, 0:NQ], in0=PTv[:, :, :, 0:NQ], in1=mk, op=ALU.mult),
                  r=['PT%d' % s, 'maskT'], w=['PT%d' % s])

            def pv(E):
                ins = None
                for e in range(2):
                    for c, (Vt, nk) in enumerate(((V0, 128), (V1, nk1))):
                        for (qs_, osl) in pvlist:
                            ins = E.matmul(out=XY[e][:, osl], lhsT=Vt[0:nk, e, :], rhs=PTv[0:nk, c, e, qs_], start=False, stop=False,
                                           skip_group_check=True)
                return ins
            pending_pv.append(lambda: S.add('pe', pv, r=['PT%d' % s] + ['Vd%d_%s' % (vs, k) for k in vkeys], w=['acc']))
            while len(pending_pv) > 2:
                pending_pv.pop(0)()

        S.add('dve', lambda E: E.memset(QsBD.rearrange("p b h q -> p (b h q)"), 0.0), w=['QsBD'])
        for b in range(4):
            S.add('dve', lambda E, b=b: E.tensor_copy(out=QsBD[0:64, b, :, 0:4], in_=QTA[0:64, :, 1024 + 4 * b:1028 + 4 * b]), r=['QTs'], w=['QsBD'])
            S.add('dve', lambda E, b=b: E.tensor_copy(out=QsBD[64:128, b, :, 4:8], in_=QTA[64:128, :, 1024 + 4 * b:1028 + 4 * b]), r=['QTs'], w=['QsBD'])
        pkt = psb[:, 7 * 1024:7 * 1024 + 5 * 128].rearrange("p (m c) -> p m c", m=5)
        pSs = ps[:, 7 * 512 + 320:7 * 512 + 400].rearrange("p (m q) -> p m q", m=10)
        pNs = ps[:, 7 * 512 + 400:7 * 512 + 408]
        pDs = ps[:, 7 * 512 + 416:7 * 512 + 424]
        SB = ['sb67']

        def sample_stages(it):
            b, hp = it // 8, it % 8
            s = it % 2
            cs = slice(hp * 128, hp * 128 + 128)

            def st_dma():
                for (dst, src, key) in ((Kc[s], ck, 'Kc%d' % s), (Vc[s], cv, 'Vc%d' % s)):
                    S.add('pool', lambda E, dst=dst, src=src: E.dma_start(out=dst[:, 0, :], in_=src[b, 1920:2048, cs]), w=[key + 'a'], dma=True)
                    S.add('pool', lambda E, dst=dst, src=src: E.dma_start(out=dst[:, 1:5, :], in_=src[b, 1536:2048, cs].rearrange("(i r) c -> i r c", r=4)), w=[key + 'b'], dma=True)
                    S.add('pool', lambda E, dst=dst, src=src: E.dma_start(out=dst[:, 5:9, :], in_=src[b, :, cs].rearrange("(i r) c -> i r c", r=16)[:, 0:4, :]), w=[key + 'c'], dma=True)

            def st_tr():
                for (m0, m1) in ((0, 5), (5, 9)):
                    def trs(E, m0=m0, m1=m1):
                        ins = None
                        for m in range(m0, m1):
                            ins = E.transpose(out=pkt[:, m - m0, :], in_=Kc[s][:, m, :], identity=ident)
                        return ins
                    S.add('pe', trs, r=['Kc%da' % s, 'Kc%db' % s, 'Kc%dc' % s, 'ident'] + SB, w=SB)
                    S.add('act', lambda E, m0=m0, m1=m1: E.activation(out=KTc[s][:, m0:m1, :], in_=pkt[:, 0:m1 - m0, :], func=AF.Copy), r=SB, w=['KTc%d' % s] + SB)

            def st_qk():
                def qks(E):
                    for m in range(9):
                        E.matmul(out=pSs[:, m, :], lhsT=KTc[s][:, m, :], rhs=QsBD[:, b, hp, :], start=True, stop=True)
                    return E.matmul(out=pSs[0:16, 9, :], lhsT=KTs[:, hp, 0:16], rhs=QsBD[:, b, hp, :], start=True, stop=True)
                S.add('pe', qks, r=['KTc%d' % s, 'QsBD', 'KTs'] + SB, w=SB)
                S.add('act', lambda E: E.activation(out=PTs[s][:, 0:9, :], in_=pSs[:, 0:9, :], func=AF.Exp, scale=SCALE), r=SB, w=['PTsa%d' % s] + SB)
                S.add('act', lambda E: E.activation(out=PTs[s][0:16, 9, :], in_=pSs[0:16, 9, :], func=AF.Exp, scale=SCALE), r=SB, w=['PTsb%d' % s] + SB)
                S.add('dve', lambda E: E.tensor_tensor(out=PTs[s][:, 0:9, :], in0=PTs[s][:, 0:9, :], in1=masks[:, b, 0:9, :], op=ALU.mult),
                      r=['PTsa%d' % s, 'masks'], w=['PTsa%d' % s])
                S.add('dve', lambda E: E.tensor_tensor(out=PTs[s][0:16, 9, :], in0=PTs[s][0:16, 9, :], in1=masks[0:16, b, 9, :], op=ALU.mult),
                      r=['PTsb%d' % s, 'masks'], w=['PTsb%d' % s])

            def st_pv():
                def pvs(E):
                    for m in range(9):
                        E.matmul(out=pNs, lhsT=Vc[s][:, m, :], rhs=PTs[s][:, m, :], start=(m == 0), stop=False)
                    E.matmul(out=pNs, lhsT=Vs[0:16, cs], rhs=PTs[s][0:16, 9, :], start=False, stop=True)
                    for m in range(9):
                        E.matmul(out=pDs, lhsT=ones, rhs=PTs[s][:, m, :], start=(m == 0), stop=False)
                    return E.matmul(out=pDs, lhsT=ones[0:16, :], rhs=PTs[s][0:16, 9, :], start=False, stop=True)
                S.add('pe', pvs, r=['PTsa%d' % s, 'PTsb%d' % s, 'Vc%da' % s, 'Vc%db' % s, 'Vc%dc' % s, 'vsamp', 'ones'] + SB, w=SB)
                S.add('dve', lambda E: E.reciprocal(out=rDs[s][:, 0:8], in_=pDs), r=SB, w=['rDs%d' % s] + SB)
                S.add('dve', lambda E: E.tensor_tensor(out=os_[s][:, 0:8], in0=pNs, in1=rDs[s][:, 0:8], op=ALU.mult), r=['rDs%d' % s] + SB, w=['os%d' % s] + SB)
                S.add('pool', lambda E: E.tensor_copy(out=QTA[0:64, hp, 1024 + 4 * b:1028 + 4 * b], in_=os_[s][0:64, 0:4]), r=['os%d' % s, 'QsBD'], w=['QTs'])
                S.add('pool', lambda E: E.tensor_copy(out=QTA[64:128, hp, 1024 + 4 * b:1028 + 4 * b], in_=os_[s][64:128, 4:8]), r=['os%d' % s, 'QsBD'], w=['QTs'])
            return [st_dma, st_tr, st_qk, st_pv]

        stq = []
        allst = [sample_stages(it) for it in range(32)]
        stq += [allst[0][0], allst[1][0]]
        for it in range(32):
            stq += allst[it][1:]
            if it + 2 < 32:
                stq.append(allst[it + 2][0])
        stq_pos = [0]
        unit_no = [0]

        fin_pending = [None]
        hp_unit = [0]

        def pump():
            hp_unit[0] += 1
            if hp_unit[0] >= 3 and hp_unit[0] % 2 == 1 and fin_pending[0]:
                fin_pending[0].pop(0)()
            unit_no[0] += 1
            if unit_no[0] % 2 == 0 and stq_pos[0] < len(stq):
                stq[stq_pos[0]]()
                stq_pos[0] += 1

        for hp in range(8):
            vs = hp % 2
            V = Vd[vs]
            def vsrc(rows):
                return vscr[rows, hp * 128:hp * 128 + 128]

            def vdst(t, NP=128):
                return t
            c0 = hp * 128
            for which in ('d1', 'd4', 'd16'):
                for e in range(2):
                    cse = slice(c0 + 64 * e, c0 + 64 * e + 64)
                    dc = slice(64 * e, 64 * e + 64)
                    if which == 'd1':
                        S.add('sp', lambda E, V=V, cse=cse, dc=dc, e=e: E.dma_start(out=V['d1'][:, :, e, dc], in_=vscr[1920:3072, cse].rearrange("(m p) c -> p m c", p=128)),
                              w=['Vd%d_1' % vs], dma=True)
                    elif which == 'd4':
                        for m in range(3):
                            S.add('sp', lambda E, V=V, cse=cse, dc=dc, e=e, m=m: E.dma_start(out=V['d4'][:, m, :, e, dc],
                                                                                              in_=vscr[1536 + 512 * m:1536 + 512 * (m + 1), cse].rearrange("(kk r) c -> kk r c", r=4)),
                                  w=['Vd%d_4%d' % (vs, m)], dma=True)
                    else:
                        S.add('sp', lambda E, V=V, cse=cse, dc=dc, e=e: E.dma_start(out=V['d16a'][:, :, e, dc], in_=vscr[0:2048, cse].rearrange("(kk r) c -> kk r c", r=16)),
                              w=['Vd%d_16a' % vs], dma=True)
                        S.add('sp', lambda E, V=V, cse=cse, dc=dc, e=e: E.dma_start(out=V['d16b'][0:64, :, e, dc], in_=vscr[2048:3072, cse].rearrange("(kk r) c -> kk r c", r=16)),
                              w=['Vd%d_16b' % vs], dma=True)
            S.add('dve', lambda E: E.memset(ps[:, 0:2048], 0.0), w=['acc', 'accX', 'accY'])
            hp_unit[0] = 0
            for n in range(8):
                attn_unit(hp, vs, slice(1920 + 128 * n, 2048 + 128 * n), slice(2048 + 128 * n, 2176 + 128 * n), 128,
                          slice(128 * n, 128 * n + 128), 128, V['d1'][:, n], V['d1'][:, n + 1],
                          [(sl(0, 64, 2), slice(64 * n, 64 * n + 64)), (sl(1, 64, 2), slice(512 + 64 * n, 512 + 64 * n + 64))], ('1',))
                pump()
            for r in range(4):
                for n in range(2):
                    k0 = 1536 + r + 512 * n
                    attn_unit(hp, vs, sl(k0, 128, 4), sl(k0 + 512, 128, 4), 128, sl(r + 512 * n, 128, 4), 128,
                              V['d4'][:, n, r], V['d4'][:, n + 1, r],
                              [(slice(0, 128), sl((r % 2) * 512 + r // 2 + 256 * n, 128, 2))], ('4%d' % n, '4%d' % (n + 1)))
                    pump()
            for r in range(16):
                attn_unit(hp, vs, sl(r, 128, 16), sl(2048 + r, 64, 16), 128, sl(r, 64, 16), 64,
                          V['d16a'][:, r], V['d16b'][:, r],
                          [(slice(0, 64), sl((r % 2) * 512 + r // 2, 64, 8))], ('16a', '16b'))
                pump()
            while pending_pv:
                pending_pv.pop(0)()
            X, Y = XY
            S.add('act', lambda E: E.activation(out=XYc[:, 0:1024], in_=X, func=AF.Copy), r=['acc'], w=['xyA', 'accX'])
            S.add('dve', lambda E: E.tensor_copy(out=XYc[:, 1024:2048], in_=Y), r=['acc'], w=['xyB', 'accY'])

            def fin_stages(hp=hp):
                def mul(e, par):
                    rows = slice(64 * e, 64 * e + 64)
                    off = 1024 * e
                    key = 'xyA' if e == 0 else 'xyB'
                    return lambda: S.add('pool', lambda E: E.tensor_tensor(out=QTA[rows, hp, sl(par, 512, 2)], in0=XYc[rows, off + par * 512:off + (par + 1) * 512],
                                                                           in1=rD[rows, par * 512:(par + 1) * 512], op=ALU.mult),
                                         r=[key, 'rDa' if e == 0 else 'rDb', 'QT'], w=['QT'])
                return [
                    lambda: S.add('act', lambda E: E.activation(out=rD[0:64, :], in_=XYc[64:128, 0:1024], func=AF.Ln), r=['xyA'], w=['rDa']),
                    lambda: S.add('act', lambda E: E.activation(out=rD[64:128, :], in_=XYc[0:64, 1024:2048], func=AF.Ln), r=['xyB'], w=['rDb']),
                    lambda: S.add('act', lambda E: E.activation(out=rD[0:64, :], in_=rD[0:64, :], func=AF.Exp, scale=-1.0), r=['rDa'], w=['rDa']),
                    lambda: S.add('act', lambda E: E.activation(out=rD[64:128, :], in_=rD[64:128, :], func=AF.Exp, scale=-1.0), r=['rDb'], w=['rDb']),
                    mul(0, 0), mul(0, 1), mul(1, 0), mul(1, 1),
                ]
            fin_pending[0] = fin_stages()
        while fin_pending[0]:
            fin_pending[0].pop(0)()
        while stq_pos[0] < len(stq):
            stq[stq_pos[0]]()
            stq_pos[0] += 1

        S.marks['A'] = len(S.ops)
        hw_marks['A'] = A.p
        S.barrier()

        A.p = base_mark
        CT = A.bf(8 * 1040).rearrange("p (h t) -> p h t", h=8)
        o2_mark = A.p
        dz0 = A.p
        xst2 = [A.f32(2048) for _ in range(2)]
        xn2 = [A.bf(2048) for _ in range(3)]
        gbc2 = A.f32(2048)
        vcg = [A.f32(1024) for _ in range(2)]
        vcc = [A.f32(1024) for _ in range(2)]
        vco = [A.f32(1024) for _ in range(2)]
        lng = A.f32(1024)
        lnb = A.f32(1024)
        assert A.p - dz0 >= 16384, (A.p - dz0)
        Wo = arena_t[:, dz0:dz0 + 16384].bitcast(BF16).rearrange("p (j c) -> p j c", j=16)
        DZKEYS = ['xst0', 'xst1', 'xn0', 'xn1', 'xn2', 'gbc', 'lng', 'lnb'] + ['%s%d' % (k, i) for k in ('vcgA', 'vcgB', 'vcc', 'vco') for i in range(2)]
        dz_end = A.p
        xnT2 = A.bf(16 * 1040).rearrange("p (j t) -> p j t", j=16)
        W2 = [A.bf(16 * 512).rearrange("p (j c) -> p j c", j=16) for _ in range(2)]
        vcn = A.bf(9 * 1024).rearrange("p (b c) -> p b c", b=9)
        WmT = A.bf(8 * 128).rearrange("p (g t) -> p g t", g=8)
        WmS = A.bf(8 * 16).rearrange("p (g t) -> p g t", g=8)
        bsb = A.f32(1024).rearrange("p (g t) -> p g t", g=8)
        sil = [A.bf(512) for _ in range(2)]
        mtmp = [A.f32(128) for _ in range(2)]
        B = dict(xnT=xnT2, Wst=W2, xst=xst2, xn=xn2, gbc=gbc2)
        xnT, Wst = xnT2, W2

        S.add('sp', lambda E: E.dma_start(out=gbc2, in_=norm_g.partition_broadcast(128)), w=['gbc'], dma=True)
        S.add('pool', lambda E: E.dma_start(out=WmT.rearrange("p g t -> p (g t)"), in_=wsT), w=['WmT'], dma=True)
        S.add('pool', lambda E: E.dma_start(out=WmS[0:16].rearrange("p g t -> p (g t)"), in_=wsS), w=['WmS'], dma=True)

        wsv = [load_w(C_VC), load_w(C_VC + 512)]
        o2s2 = {}
        o2s2[0] = build_block(*own_blocks[0], defer='manual')
        o2s2[1] = build_block(*own_blocks[1], defer='manual')
        o2s2.pop(0)()
        S.add('sp', lambda E: E.dma_start(out=lng, in_=ln_g.partition_broadcast(128)), w=['lng'], dma=True)
        S.add('sp', lambda E: E.dma_start(out=lnb, in_=ln_b.partition_broadcast(128)), w=['lnb'], dma=True)
        S.add('sp', lambda E: E.dma_start(out=bsb.rearrange("p g t -> p (g t)"), in_=b_s.partition_broadcast(128)), w=['bsb'], dma=True)
        for b in range(9):
            NT = 128 if b < 8 else 16
            tok0 = b * 128
            if b + 2 < 9:
                o2s2[b + 2] = build_block(*own_blocks[b + 2], defer='manual')
            if b + 1 in o2s2:
                o2s2.pop(b + 1)()
            v = b % 2
            c0 = 48 + 8 * v
            g_, c_, o_ = vcg[v], vcc[v], vco[v]
            for h in range(2):
                bank, bkey = nat_matmul(wsv[h], NT, tok0)
                S.add('act', lambda E, bank=bank, NT=NT, h=h, g_=g_, c0=c0: E.activation(out=g_[0:NT, h * 512:(h + 1) * 512], in_=bank[0:NT, :], func=AF.Gelu_apprx_tanh,
                                                                                   accum_out=sm[0:NT, c0 + h:c0 + h + 1]),
                      r=[bkey], w=['vcg%s%d' % ('AB'[h], v), 'vsum%d%d' % (h, v)])
            S.add('dve', lambda E, NT=NT, c0=c0: E.tensor_tensor(out=sm[0:NT, c0 + 2:c0 + 3], in0=sm[0:NT, c0:c0 + 1], in1=sm[0:NT, c0 + 1:c0 + 2], op=ALU.add),
                  r=['vsum0%d' % v, 'vsum1%d' % v], w=['vmean%d' % v])
            S.add('dve', lambda E, NT=NT, c0=c0: E.tensor_scalar(out=sm[0:NT, c0 + 3:c0 + 4], in0=sm[0:NT, c0 + 2:c0 + 3], scalar1=1.0 / 1024, scalar2=None, op0=ALU.mult),
                  r=['vmean%d' % v], w=['vmean2%d' % v])
            S.add('dve', lambda E, NT=NT, c0=c0, g_=g_, c_=c_: E.tensor_scalar(out=c_[0:NT, :], in0=g_[0:NT, :], scalar1=sm[0:NT, c0 + 3:c0 + 4], scalar2=None, op0=ALU.subtract),
                  r=['vcgA%d' % v, 'vcgB%d' % v, 'vmean2%d' % v], w=['vcc%d' % v])
            S.add('dve', lambda E, NT=NT, c0=c0, c_=c_, o_=o_: E.scalar_tensor_tensor(out=o_[0:NT, :], in0=c_[0:NT, :], scalar=1.0, in1=c_[0:NT, :], op0=ALU.mult, op1=ALU.mult,
                                                                                    accum_out=sm[0:NT, c0 + 4:c0 + 5]),
                  r=['vcc%d' % v], w=['vco%d' % v, 'vss%d' % v])
            S.add('dve', lambda E, NT=NT, c0=c0: E.tensor_scalar(out=sm[0:NT, c0 + 5:c0 + 6], in0=sm[0:NT, c0 + 4:c0 + 5], scalar1=1.0 / 1024, scalar2=EPS, op0=ALU.mult, op1=ALU.add),
                  r=['vss%d' % v], w=['vvar%d' % v])
            S.add('act', lambda E, NT=NT, c0=c0: E.activation(out=sm[0:NT, c0 + 6:c0 + 7], in_=sm[0:NT, c0 + 5:c0 + 6], func=AF.Sqrt), r=['vvar%d' % v], w=['vsd%d' % v])
            S.add('dve', lambda E, NT=NT, c0=c0: E.reciprocal(out=sm[0:NT, c0 + 7:c0 + 8], in_=sm[0:NT, c0 + 6:c0 + 7]), r=['vsd%d' % v], w=['vrstd%d' % v])
            S.add('dve', lambda E, NT=NT, c0=c0, c_=c_, o_=o_: E.scalar_tensor_tensor(out=o_[0:NT, :], in0=c_[0:NT, :], scalar=sm[0:NT, c0 + 7:c0 + 8], in1=lng[0:NT, :], op0=ALU.mult, op1=ALU.mult),
                  r=['vcc%d' % v, 'vrstd%d' % v, 'lng'], w=['vco%d' % v])
            if b < 8:
                S.add('pool', lambda E, NT=NT, b=b, o_=o_: E.tensor_tensor(out=vcn[0:NT, b, :], in0=o_[0:NT, :], in1=lnb[0:NT, :], op=ALU.add), r=['vco%d' % v, 'lnb'], w=['vcn'])
            else:
                S.add('pool', lambda E, NT=NT, o_=o_: E.tensor_tensor(out=o_[0:NT, :], in0=o_[0:NT, :], in1=lnb[0:NT, :], op=ALU.add), r=['vco%d' % v, 'lnb'], w=['vco%d' % v])
                S.add('act', lambda E, NT=NT, b=b, o_=o_: E.activation(out=vcn[0:NT, b, :], in_=o_[0:NT, :], func=AF.Copy), r=['vco%d' % v], w=['vcn'])
            if b == 8:
                S.add('sp', lambda E, o_=o_: E.dma_start(out=vc_samp[:, :], in_=o_[0:16, :]), r=['vco%d' % v], dma=True)

        tcnt = [0]
        tpieces = [(cbase + half * 512, kind, half) for (cbase, kind) in ((C_U, 'u'), (C_GB, 'gb'), (C_GA, 'ga')) for half in range(2)]
        ws_next = load_w(tpieces[0][0])
        for pi, (pc0, kind, half) in enumerate(tpieces):
            if True:
                ws = ws_next
                if pi + 1 < len(tpieces):
                    ws_next = load_w(tpieces[pi + 1][0])
                if pi < 4:
                    S.add('pool', lambda E, q=pi: E.dma_start(out=Wo[:, :, q * 512:(q + 1) * 512], in_=w_out_v[:, :, q * 512:(q + 1) * 512]),
                          w=['Wo%d' % pi] + DZKEYS, dma=True)
                for cb in range(4):
                    hidx = half * 4 + cb
                    for (t0, nt) in ((0, 512), (512, 512), (1024, 16)):
                        s = tcnt[0] % 2
                        tcnt[0] += 1
                        bank = ps[:, (4 + s) * 512:(5 + s) * 512]

                        def mm(E, ws=ws, cb=cb, t0=t0, nt=nt, bank=bank):
                            ins = None
                            for j in range(16):
                                ins = E.matmul(out=bank[:, 0:nt], lhsT=Wst[ws][:, j, cb * 128:(cb + 1) * 128], rhs=xnT[:, j, t0:t0 + nt], start=(j == 0), stop=(j == 15))
                            return ins
                        S.add('pe', mm, r=['xnT_b%d%s' % (bb, ab) for bb in range(t0 // 128, (t0 + nt + 127) // 128) for ab in 'ab'] + ['W%d%s' % (ws, 'a' if cb < 2 else 'b')], w=['nat%d' % s])
                        if kind == 'u':
                            S.add('act', lambda E, bank=bank, hidx=hidx, t0=t0, nt=nt: E.activation(out=CT[:, hidx, t0:t0 + nt], in_=bank[:, 0:nt], func=AF.Gelu_apprx_tanh),
                                  r=['nat%d' % s], w=['CT'])
                        else:
                            dstT = CT if kind == 'gb' else QTA
                            dkey = 'CT' if kind == 'gb' else 'QT'
                            S.add('act', lambda E, bank=bank, s=s, nt=nt: E.activation(out=sil[s][:, 0:nt], in_=bank[:, 0:nt], func=AF.Silu), r=['nat%d' % s], w=['sil%d' % s])
                            S.add('dve', lambda E, dstT=dstT, hidx=hidx, t0=t0, nt=nt, s=s: E.tensor_tensor(out=dstT[:, hidx, t0:t0 + nt], in0=dstT[:, hidx, t0:t0 + nt],
                                                                                                               in1=sil[s][:, 0:nt], op=ALU.mult),
                                  r=['sil%d' % s, dkey], w=[dkey])

        for b in range(9):
            NT = 128 if b < 8 else 16
            tok0 = b * 128
            s2 = b % 2
            pmb = ps[:, (4 + 2 * s2) * 512:(4 + 2 * s2) * 512 + 8 * NT]
            pm3 = pmb.rearrange("p (g t) -> p g t", g=8)

            def mmc(E, b=b, NT=NT, pm3=pm3):
                ins = None
                for g in range(8):
                    if b < 8:
                        ins = E.matmul(out=pm3[:, g, :], lhsT=vcn[:, b, g * 128:(g + 1) * 128], rhs=WmT[:, g, :], start=True, stop=True)
                    else:
                        ins = E.matmul(out=pm3[:, g, :], lhsT=vcn[0:16, 8, g * 128:(g + 1) * 128], rhs=WmS[0:16, g, :], start=True, stop=True)
                return ins
            S.add('pe', mmc, r=['vcn', 'WmT', 'WmS'], w=['pmb%d' % s2] + (['nat0', 'nat1'] if s2 == 0 else ['pk0', 'pk1']))
            if b < 8:
                S.add('dve', lambda E, pm3=pm3: E.tensor_tensor(out=pm3, in0=pm3, in1=bsb, op=ALU.add), r=['pmb%d' % s2, 'bsb'], w=['pmb%d' % s2])
            else:
                pm4 = pmb.rearrange("p (g b t) -> p g b t", g=8, b=4)
                S.add('dve', lambda E, pm4=pm4: E.tensor_tensor(out=pm4, in0=pm4, in1=bsb[:, :, 0:4].unsqueeze(2).to_broadcast([128, 8, 4, 4]), op=ALU.add),
                      r=['pmb%d' % s2, 'bsb'], w=['pmb%d' % s2])
            S.add('dve', lambda E, pm3=pm3, tok0=tok0, NT=NT: E.tensor_tensor(out=CT[:, :, tok0:tok0 + NT], in0=pm3, in1=CT[:, :, tok0:tok0 + NT], op=ALU.mult),
                  r=['pmb%d' % s2, 'CT'], w=['CT'])

        S.marks['O2'] = len(S.ops)
        hw_marks['O2'] = A.p
        S.barrier()

        A.p = dz_end
        xst3 = [A.f32(2048) for _ in range(2)]
        yf = [A.f32(2048) for _ in range(2)]
        fgb = A.f32(2048)
        S.add('sp', lambda E: E.dma_start(out=fgb, in_=final_g.partition_broadcast(128)), w=['fgb'], dma=True)
        for bi, b in enumerate([8, 0, 1, 2, 3, 4, 5, 6, 7]):
            NT = 128 if b < 8 else 16
            tok0 = b * 128
            s = bi % 2
            src = x_own[tok0:tok0 + 128, :] if b < 8 else x_samp[:, :]
            dst = y_own[tok0:tok0 + 128, :] if b < 8 else y_samp[:, :]
            S.add('sp', lambda E, s=s, src=src, NT=NT: E.dma_start(out=xst3[s][0:NT, :], in_=src), w=['x3_%d' % s], dma=True)
            po = ps[:, s * 2048:(s + 1) * 2048]

            def mmo(E, NT=NT, tok0=tok0, po=po):
                ins = None
                for q in range(4):
                    for j in range(16):
                        lh = QTA[:, j, tok0:tok0 + NT] if j < 8 else CT[:, j - 8, tok0:tok0 + NT]
                        ins = E.matmul(out=po[0:NT, q * 512:(q + 1) * 512], lhsT=lh, rhs=Wo[:, j, q * 512:(q + 1) * 512], start=(j == 0), stop=(j == 15))
                return ins
            S.add('pe', mmo, r=['QT', 'CT', 'Wo0', 'Wo1', 'Wo2', 'Wo3'], w=['po%d' % s])
            S.add('dve', lambda E, s=s, NT=NT, po=po: E.tensor_tensor(out=yf[s][0:NT, :], in0=po[0:NT, :], in1=xst3[s][0:NT, :], op=ALU.add),
                  r=['po%d' % s, 'x3_%d' % s], w=['yf%d' % s])
            c = 20 + 4 * s
            S.add('act', lambda E, s=s, NT=NT, c=c: E.activation(out=xst3[s][0:NT, :], in_=yf[s][0:NT, :], func=AF.Square, accum_out=sm[0:NT, c:c + 1]),
                  r=['yf%d' % s], w=['x3_%d' % s, 'yss%d' % s])
            S.add('dve', lambda E, NT=NT, c=c: E.tensor_scalar(out=sm[0:NT, c + 1:c + 2], in0=sm[0:NT, c:c + 1], scalar1=1.0 / D, scalar2=EPS, op0=ALU.mult, op1=ALU.add),
                  r=['yss%d' % s], w=['yms%d' % s])
            S.add('act', lambda E, NT=NT, c=c: E.activation(out=sm[0:NT, c + 2:c + 3], in_=sm[0:NT, c + 1:c + 2], func=AF.Sqrt), r=['yms%d' % s], w=['ysd%d' % s])
            S.add('dve', lambda E, NT=NT, c=c: E.reciprocal(out=sm[0:NT, c + 3:c + 4], in_=sm[0:NT, c + 2:c + 3]), r=['ysd%d' % s], w=['yr%d' % s])
            S.add('dve', lambda E, s=s, NT=NT, c=c: E.scalar_tensor_tensor(out=yf[s][0:NT, :], in0=yf[s][0:NT, :], scalar=sm[0:NT, c + 3:c + 4], in1=fgb[0:NT, :],
                                                                          op0=ALU.mult, op1=ALU.mult),
                  r=['yf%d' % s, 'yr%d' % s, 'fgb'], w=['yf%d' % s])
            S.add('sp', lambda E, s=s, NT=NT, dst=dst: E.dma_start(out=dst, in_=yf[s][0:NT, :]), r=['yf%d' % s], dma=True)

        hw_marks['P'] = A.p
        if TRUNC is not None:
            tr = str(TRUNC)
            if '+' in tr:
                a, b = tr.split('+')
                n = S.marks[a] + int(b)
            else:
                n = S.marks[tr]
            print("TRUNC at", n, "of", len(S.ops), S.marks)
            S.ops = S.ops[:n]
        S.emit(nc, block, sems, dma_sems)
    return nc


def _host_constants(c):
    half = 32
    inv = np.exp(-np.log(np.float32(10000.0)) * np.arange(half, dtype=np.float32) / half).astype(np.float32)
    tab = np.zeros((128, 25, 128), np.float32)
    p = np.arange(128)
    for blk in range(25):
        if blk < 16:
            pos = 1024 * c - 2048 + 128 * blk + p
        elif blk < 24:
            pos = 1024 * c + 128 * (blk - 16) + p
        else:
            pos = 16384 + (p % 4)
        ang = (pos.astype(np.float32)[:, None] * inv[None, :]).astype(np.float32)
        cs, sn = np.cos(ang).astype(np.float32), np.sin(ang).astype(np.float32)
        tab[:, blk, 0:32] = cs
        tab[:, blk, 32:64] = cs
        tab[:, blk, 64:96] = -sn
        tab[:, blk, 96:128] = sn
    valid = np.ones(3072, np.float32)
    gpos = 1024 * c - 2048 + np.arange(3072)
    valid[gpos < 0] = 0.0
    vp = np.zeros((128, 53, 64), np.float32)
    kk = np.arange(128)
    for m in range(9):
        vp[:, m, :] = valid[1920 + 128 * m + kk][:, None]
    for m in range(3):
        for r in range(4):
            vp[:, 9 + m * 4 + r, :] = valid[1536 + 512 * m + r + 4 * kk][:, None]
    for r in range(16):
        vp[:, 21 + r, :] = valid[r + 16 * kk][:, None]
        vp[0:64, 37 + r, :] = valid[2048 + r + 16 * kk[:64]][:, None]
    return tab.reshape(128, -1), vp.reshape(128, -1)


def _static_constants():
    ident = np.eye(128, dtype=np.float32)
    ones = np.ones((128, 128), np.float32)
    i = np.arange(128)[:, None]
    q = np.arange(128)[None, :]
    maskT = np.zeros((128, 2, 128), np.float32)
    maskT[:, 0, :] = (i >= q)
    maskT[:, 1, :] = (i <= q)
    masks = np.zeros((128, 4, 10, 8), np.float32)
    t = np.arange(4)
    for b in range(4):
        for e in range(2):
            masks[:, b, 0, 4 * e:4 * e + 4] = (np.arange(128)[:, None] >= t[None, :])
            for r in range(4):
                masks[:, b, 1 + r, 4 * e + r] = 1.0
                masks[:, b, 5 + r, 4 * e + r] = 1.0
            for s_ in range(4):
                for tt in range(4):
                    masks[4 * b + s_, b, 9, 4 * e + tt] = float(s_ <= tt) + 2.0 * float(s_ == tt)
    return ident, ones, maskT.reshape(128, -1), masks.reshape(128, -1)


_CACHE = {}


def kernel(x_prompt, x_sample, cache_k, cache_v, norm_g, w_in, ln_g, ln_b, w_s, b_s, w_out, final_g):
    f = np.float32
    x_prompt = np.asarray(x_prompt, f)
    x_sample = np.asarray(x_sample, f)
    cache_k = np.asarray(cache_k, f)
    cache_v = np.asarray(cache_v, f)
    w_in2 = np.ascontiguousarray(np.asarray(w_in, f)[0])
    w_out2 = np.ascontiguousarray(np.asarray(w_out, f)[0])
    ws = np.asarray(w_s, f)[0]
    wm = np.tril(ws)
    wsT = np.ascontiguousarray(np.transpose(wm, (2, 0, 1))).reshape(128, 8 * 128)
    wsS = np.zeros((16, 8, 16), f)
    for b in range(4):
        wsS[4 * b:4 * b + 4, :, 4 * b:4 * b + 4] = np.transpose(wm[:, 0:4, 0:4], (2, 0, 1))
    wsS = wsS.reshape(16, 8 * 16)
    ident, ones, maskT, masks = _static_constants()
    zeros_h = np.zeros((128, 4096), f)

    if 'nc' not in _CACHE:
        _CACHE['nc'] = build_program()
    nc = _CACHE['nc']

    in_maps = []
    xp = x_prompt[0]
    for c in range(NCORES):
        tab, vp = _host_constants(c)
        xh = np.zeros((THALO, D), f)
        lo = 1024 * c - 2048
        if lo >= 0:
            xh[:] = xp[lo:lo + 2048]
        elif lo + 2048 > 0:
            xh[-(lo + 2048):] = xp[0:lo + 2048]
        in_maps.append(dict(
            x_own=np.ascontiguousarray(xp[1024 * c:1024 * (c + 1)]),
            x_halo=xh,
            x_samp=np.ascontiguousarray(x_sample[4 * c:4 * c + 4].reshape(16, D)),
            ck=np.ascontiguousarray(cache_k[0, 4 * c:4 * c + 4].reshape(4, 2048, 1024)),
            cv=np.ascontiguousarray(cache_v[0, 4 * c:4 * c + 4].reshape(4, 2048, 1024)),
            w_in=w_in2, w_out=w_out2,
            norm_g=np.ascontiguousarray(np.asarray(norm_g, f)[0]),
            final_g=np.ascontiguousarray(np.asarray(final_g, f)),
            ln_g=np.ascontiguousarray(np.asarray(ln_g, f)[0]),
            ln_b=np.ascontiguousarray(np.asarray(ln_b, f)[0]),
            b_s=np.ascontiguousarray(np.asarray(b_s, f)[0].reshape(1024)),
            wsT=wsT, wsS=wsS, tab=tab, ident=ident, ones=ones, maskT=maskT, masks=masks, validp=vp, zeros=zeros_h,
        ))
    res = run_bass_kernel_spmd(nc, in_maps, core_ids=list(range(NCORES)))
    R = res.results
    y_prompt = np.concatenate([R[c]["y_own"] for c in range(NCORES)], 0).reshape(1, 8192, D)
    y_sample = np.concatenate([R[c]["y_samp"].reshape(4, 4, D) for c in range(NCORES)], 0)
    k_tail = np.concatenate([R[6]["k_own"], R[7]["k_own"]], 0).reshape(1, 1, 2048, 16, 64)
    v_tail = np.concatenate([R[6]["v_own"], R[7]["v_own"]], 0).reshape(1, 1, 2048, 16, 64)
    k_new = np.concatenate([R[c]["k_samp"].reshape(4, 4, 16, 64) for c in range(NCORES)], 0)[None]
    v_new = np.concatenate([R[c]["v_samp"].reshape(4, 4, 16, 64) for c in range(NCORES)], 0)[None]
    vc_new = np.concatenate([R[c]["vc_samp"].reshape(4, 4, 1024) for c in range(NCORES)], 0)[None]
    return (y_prompt.astype(f), y_sample.astype(f), k_tail.astype(f), v_tail.astype(f),
            k_new.astype(f), v_new.astype(f), vc_new.astype(f))
```

```python
import numpy as np
import concourse.bass as bass
import concourse.mybir as mybir
from concourse.bass_utils import run_bass_kernel_spmd

F32, BF16 = mybir.dt.float32, mybir.dt.bfloat16
AF = mybir.ActivationFunctionType
ALU = mybir.AluOpType

NCORES = 8
TRUNC = None
D = 2048
DIN = 7168
TOWN = 1024
THALO = 2048
EPS = 1e-6
SCALE = 0.125
C_Q, C_K, C_V, C_GA, C_U, C_VC, C_GB = 0, 1024, 2048, 3072, 4096, 5120, 6144


class Sched:
    def __init__(self):
        self.ops = []
        self.last_w = {}
        self.readers = {}
        self.bar = None
        self.seen_bar = {}
        self.marks = {}

    def add(self, eng, fn, r=(), w=(), dma=False):
        deps = set()
        for k in r:
            if k in self.last_w:
                deps.add(self.last_w[k])
        for k in w:
            if k in self.last_w:
                deps.add(self.last_w[k])
            deps.update(self.readers.get(k, ()))
        if self.bar is not None and not self.seen_bar.get(eng):
            deps.update(self.bar)
            self.seen_bar[eng] = True
        i = len(self.ops)
        self.ops.append(dict(eng=eng, fn=fn, deps=deps, dma=dma))
        for k in r:
            self.readers.setdefault(k, []).append(i)
        for k in w:
            self.last_w[k] = i
            self.readers[k] = []
        return i

    def barrier(self):
        last = {}
        dmas = []
        for i, o in enumerate(self.ops):
            if o['dma']:
                dmas.append(i)
            else:
                last[o['eng']] = i
        self.bar = set(last.values()) | set(dmas[-64:])
        self.seen_bar = {}

    def emit(self, nc, block, sems, dma_sems):
        ops = self.ops
        NS = len(next(iter(dma_sems.values())))
        need_sig = [False] * len(ops)
        for i, o in enumerate(ops):
            for d in o['deps']:
                po = ops[d]
                if po['dma']:
                    continue
                if po['eng'] == o['eng'] and o['eng'] == 'pe' and not o['dma']:
                    continue
                need_sig[d] = True
        sigval = [None] * len(ops)
        cnt = {e: 0 for e in sems}
        dcnt = {e: 0 for e in dma_sems}
        for i, o in enumerate(ops):
            e = o['eng']
            if o['dma']:
                k = dcnt[e]
                dcnt[e] += 1
                sigval[i] = ('d', e, k % NS, 16 * (k // NS + 1))
            elif need_sig[i]:
                cnt[e] += 1
                sigval[i] = ('c', e, None, cnt[e])
        final_dma = {}
        for i, o in enumerate(ops):
            if o['dma']:
                _, e, s, v = sigval[i]
                final_dma[(e, s)] = v

        def run(eng_name, E):
            waited = {}

            def wait(key, sem, val):
                if waited.get(key, 0) < val:
                    E.wait_ge(sem, val)
                    waited[key] = val

            for i, o in enumerate(ops):
                if o['eng'] != eng_name:
                    continue
                for d in sorted(o['deps']):
                    sv = sigval[d]
                    if sv is None:
                        continue
                    kind, pe_, s, v = sv
                    if kind == 'd':
                        wait(('d', pe_, s), dma_sems[pe_][s], v)
                    else:
                        if pe_ == eng_name and eng_name == 'pe' and not o['dma']:
                            continue
                        wait(('c', pe_), sems[pe_], v)
                if o['dma']:
                    _, e, s, v = sigval[i]
                    if v > 16:
                        wait(('d', e, s), dma_sems[e][s], v - 16)
                    ins = o['fn'](E)
                    ins.then_inc(dma_sems[e][s], 16)
                else:
                    ins = o['fn'](E)
                    if sigval[i] is not None:
                        ins.then_inc(sems[eng_name], 1)
            if eng_name in dma_sems:
                for (e, s), v in final_dma.items():
                    if e == eng_name:
                        wait(('d', e, s), dma_sems[e][s], v)

        @block.tensor
        def _(E):
            run('pe', E)

        @block.scalar
        def _(E):
            run('act', E)

        @block.vector
        def _(E):
            run('dve', E)

        @block.gpsimd
        def _(E):
            run('pool', E)

        @block.sync
        def _(E):
            run('sp', E)


def sl(start, n, step):
    return slice(start, start + (n - 1) * step + 1, step)


class Arena:
    def __init__(self, ap, nwords):
        self.ap = ap
        self.n = nwords
        self.p = 0

    def f32(self, n):
        self.p = (self.p + 15) // 16 * 16
        a = self.ap[:, self.p:self.p + n]
        self.p += n
        assert self.p <= self.n, ("arena overflow", self.p, self.n)
        return a

    def bf(self, n):
        assert n % 2 == 0
        return self.f32(n // 2).bitcast(BF16)


def build_program():
    nc = bass.Bass("TRN2", target_bir_lowering=False)

    def din(name, shape, dt=F32):
        return nc.dram_tensor(name, list(shape), dt, kind="ExternalInput").ap()

    def dout(name, shape, dt=F32):
        return nc.dram_tensor(name, list(shape), dt, kind="ExternalOutput").ap()

    x_own = din("x_own", [TOWN, D])
    x_halo = din("x_halo", [THALO, D])
    x_samp = din("x_samp", [16, D])
    ck = din("ck", [4, 2048, 1024])
    cv = din("cv", [4, 2048, 1024])
    w_in = din("w_in", [D, DIN])
    w_out = din("w_out", [D, D])
    norm_g = din("norm_g", [D])
    final_g = din("final_g", [D])
    ln_g = din("ln_g", [1024])
    ln_b = din("ln_b", [1024])
    b_s = din("b_s", [1024])
    wsT = din("wsT", [128, 8 * 128])
    wsS = din("wsS", [16, 8 * 16])
    tab_d = din("tab", [128, 25 * 128])
    ident_d = din("ident", [128, 128])
    ones_d = din("ones", [128, 128])
    maskT_d = din("maskT", [128, 256])
    masks_d = din("masks", [128, 4 * 10 * 8])
    validp_d = din("validp", [128, 53 * 64])

    y_own = dout("y_own", [TOWN, D])
    y_samp = dout("y_samp", [16, D])
    k_own = dout("k_own", [TOWN, 1024])
    v_own = dout("v_own", [TOWN, 1024])
    k_samp = dout("k_samp", [16, 1024])
    v_samp = dout("v_samp", [16, 1024])
    vc_samp = dout("vc_samp", [16, 1024])

    vscr = nc.dram_tensor("vscr", [3072, 1024], BF16, kind="Internal").ap()

    w_in_v = w_in.rearrange("(j p) c -> p j c", p=128)
    w_out_v = w_out.rearrange("(j p) c -> p j c", p=128)

    S = Sched()
    NW = 53200

    from contextlib import ExitStack
    with ExitStack() as es:
        arena_t = es.enter_context(nc.sbuf_tensor("arena", [128, NW], F32))
        ps = es.enter_context(nc.psum_tensor("ps", [128, 4096], F32))
        sems = {e: es.enter_context(nc.semaphore("s_" + e)) for e in ('pe', 'act', 'dve', 'pool')}
        NS = 16
        dma_sems = {e: [es.enter_context(nc.semaphore("d_%s%d" % (e, i))) for i in range(NS)]
                    for e in ('sp', 'pool', 'act')}
        block = es.enter_context(nc.Block())

        A = Arena(arena_t, NW)
        psb = ps[:, :].bitcast(BF16)

        ident = A.bf(128)
        ones = A.bf(128)
        maskT = A.bf(256).rearrange("p (c q) -> p c q", c=2)
        tab = A.f32(25 * 128).rearrange("p (b f) -> p b f", b=25)
        sm = A.f32(64)
        QTA = A.bf(8 * 1040).rearrange("p (h t) -> p h t", h=8)
        base_mark = A.p

        S.add('pool', lambda E: E.dma_start(out=ident, in_=ident_d), w=['ident'], dma=True)
        S.add('pool', lambda E: E.dma_start(out=ones, in_=ones_d), w=['ones'], dma=True)
        S.add('pool', lambda E: E.dma_start(out=maskT.rearrange("p c q -> p (c q)"), in_=maskT_d), w=['maskT'], dma=True)

        kt_off = (A.p + 15) // 16 * 16
        KT = A.bf(8 * 3072).rearrange("p (h t) -> p h t", h=8)
        KTs = A.bf(8 * 16).rearrange("p (h t) -> p h t", h=8)
        Vs = A.bf(1024)
        KText = arena_t[:, kt_off:A.p].bitcast(BF16)
        ph_mark = A.p
        B = dict(xnT=A.bf(16 * 1040).rearrange("p (j t) -> p j t", j=16),
                 Wst=[A.bf(16 * 512).rearrange("p (j c) -> p j c", j=16) for _ in range(2)],
                 xst=[A.f32(2048) for _ in range(2)],
                 xn=[A.bf(2048) for _ in range(3)],
                 gbc=A.f32(2048))
        raw = [A.f32(512) for _ in range(2)]
        rt = [A.f32(512) for _ in range(2)]
        rtmp = [A.f32(512) for _ in range(2)]
        rof = [A.f32(512) for _ in range(2)]
        rob = [A.bf(512) for _ in range(3)]
        vbf = [A.bf(512) for _ in range(4)]
        vf = [A.f32(512) for _ in range(2)]

        S.add('sp', lambda E, g=B['gbc']: E.dma_start(out=g, in_=norm_g.partition_broadcast(128)), w=['gbc'], dma=True)

        cnt = dict(x=0, w=0, nat=0, tr=0, raw=0, ro=0, v=0, kt=0)
        deferred = []

        def flush_deferred(keep=0):
            while len(deferred) > keep:
                deferred.pop(0)()

        build_q = []

        def build_block(src, NT, tok0, defer=0):
            xnT, xst, xn, gbc = B['xnT'], B['xst'], B['xn'], B['gbc']
            s = cnt['x'] % 2
            n3 = cnt['x'] % 3
            cnt['x'] += 1
            xs = xst[s]
            xnb = xn[n3]
            c0 = 32 + 4 * n3
            bk = 'xnT_b%d' % (tok0 // 128)
            S.add('sp', lambda E: E.dma_start(out=xs[0:NT, :], in_=src), w=['xst%d' % s], dma=True)

            def stage1():
                S.add('act', lambda E: E.activation(out=xnb[0:NT, :], in_=xs[0:NT, :], func=AF.Square, accum_out=sm[0:NT, c0:c0 + 1]),
                      r=['xst%d' % s], w=['xn%d' % n3, 'ssq%d' % n3])
                S.add('dve', lambda E: E.tensor_scalar(out=sm[0:NT, c0 + 1:c0 + 2], in0=sm[0:NT, c0:c0 + 1], scalar1=1.0 / D, scalar2=EPS, op0=ALU.mult, op1=ALU.add),
                      r=['ssq%d' % n3], w=['ms%d' % n3])
                S.add('act', lambda E: E.activation(out=sm[0:NT, c0 + 2:c0 + 3], in_=sm[0:NT, c0 + 1:c0 + 2], func=AF.Sqrt), r=['ms%d' % n3], w=['sd%d' % n3])
                S.add('dve', lambda E: E.reciprocal(out=sm[0:NT, c0 + 3:c0 + 4], in_=sm[0:NT, c0 + 2:c0 + 3]), r=['sd%d' % n3], w=['rstd%d' % n3])
                S.add('dve', lambda E: E.scalar_tensor_tensor(out=xnb[0:NT, :], in0=xs[0:NT, :], scalar=sm[0:NT, c0 + 3:c0 + 4], in1=gbc[0:NT, :],
                                                              op0=ALU.mult, op1=ALU.mult),
                      r=['xst%d' % s, 'rstd%d' % n3, 'gbc'], w=['xn%d' % n3])
            if defer != 'manual3':
                stage1()

            def stage2():
                tp = cnt['tr'] % 2
                cnt['tr'] += 1
                pst = psb[:, tp * 2048:(tp + 1) * 2048].rearrange("p (j t) -> p j t", j=16)

                def tr(E):
                    ins = None
                    for j in range(16):
                        ins = E.transpose(out=pst[:, j, 0:NT], in_=xnb[0:NT, j * 128:(j + 1) * 128], identity=ident[0:NT, 0:NT])
                    return ins
                S.add('pe', tr, r=['xn%d' % n3, 'ident'], w=['pst%da' % tp, 'pst%db' % tp])
                S.add('act', lambda E: E.activation(out=xnT[:, 0:8, tok0:tok0 + NT], in_=pst[:, 0:8, 0:NT], func=AF.Copy), r=['pst%da' % tp], w=[bk + 'a'])
                S.add('dve', lambda E: E.tensor_copy(out=xnT[:, 8:16, tok0:tok0 + NT], in_=pst[:, 8:16, 0:NT]), r=['pst%db' % tp], w=[bk + 'b'])
            if defer == 'manual':
                return stage2
            if defer == 'manual3':
                return stage1, stage2
            if defer:
                build_q.append(stage2)
                while len(build_q) > defer:
                    build_q.pop(0)()
            else:
                stage2()

        def flush_builds():
            while build_q:
                build_q.pop(0)()

        def load_w(c0, eng='pool'):
            s = cnt['w'] % 2
            cnt['w'] += 1
            Wst = B['Wst']
            S.add(eng, lambda E, s=s, c0=c0: E.dma_start(out=Wst[s][:, :, 0:256], in_=w_in_v[:, :, c0:c0 + 256]), w=['W%da' % s], dma=True)
            S.add(eng, lambda E, s=s, c0=c0: E.dma_start(out=Wst[s][:, :, 256:512], in_=w_in_v[:, :, c0 + 256:c0 + 512]), w=['W%db' % s], dma=True)
            return s

        def nat_matmul(ws, NT, tok0):
            s = cnt['nat'] % 2
            cnt['nat'] += 1
            xnT, Wst = B['xnT'], B['Wst']
            bank = ps[:, (4 + s) * 512:(5 + s) * 512]
            bk = 'xnT_b%d' % (tok0 // 128)

            def mm(E, ws=ws, NT=NT, tok0=tok0, bank=bank):
                ins = None
                for j in range(16):
                    ins = E.matmul(out=bank[0:NT, :], lhsT=xnT[:, j, tok0:tok0 + NT], rhs=Wst[ws][:, j, :], start=(j == 0), stop=(j == 15))
                return ins
            S.add('pe', mm, r=[bk + 'a', bk + 'b', 'W%da' % ws, 'W%db' % ws], w=['nat%d' % s])
            return bank, 'nat%d' % s

        def rope_bank(bank, bkey, NT, tblk, out_f32_dram=None):
            s = cnt['raw'] % 2
            cnt['raw'] += 1
            rw = raw[s]
            rts, rtm = rt[s], rtmp[s]
            S.add('act', lambda E: E.activation(out=rw[0:NT, :], in_=bank[0:NT, :], func=AF.Copy), r=[bkey], w=['raw%d' % s])
            rw4 = rw.rearrange("p (h two f) -> p h two f", h=8, two=2)
            rt3 = rts.rearrange("p (h f) -> p h f", h=8)
            tm4 = rtm.rearrange("p (h two f) -> p h two f", h=8, two=2)
            Cb = tab[0:NT, tblk, 0:64].unsqueeze(1).to_broadcast([NT, 8, 64])
            SNb = tab[0:NT, tblk, 64:96].unsqueeze(1).to_broadcast([NT, 8, 32])
            SPb = tab[0:NT, tblk, 96:128].unsqueeze(1).to_broadcast([NT, 8, 32])
            S.add('pool', lambda E: E.tensor_tensor(out=rt3[0:NT], in0=rw.rearrange("p (h f) -> p h f", h=8)[0:NT], in1=Cb, op=ALU.mult),
                  r=['raw%d' % s, 'tab'], w=['rt%d' % s])
            S.add('dve', lambda E: E.tensor_tensor(out=tm4[0:NT, :, 0, :], in0=rw4[0:NT, :, 1, :], in1=SNb, op=ALU.mult),
                  r=['raw%d' % s, 'tab'], w=['rtmpA%d' % s])
            S.add('dve', lambda E: E.tensor_tensor(out=tm4[0:NT, :, 1, :], in0=rw4[0:NT, :, 0, :], in1=SPb, op=ALU.mult),
                  r=['raw%d' % s, 'tab'], w=['rtmpB%d' % s])
            o = cnt['ro'] % 2
            ob = cnt['ro'] % 3
            cnt['ro'] += 1
            S.add('dve', lambda E: E.tensor_tensor(out=rob[ob][0:NT, :], in0=rts[0:NT, :], in1=rtm[0:NT, :], op=ALU.add),
                  r=['rt%d' % s, 'rtmpA%d' % s, 'rtmpB%d' % s], w=['rob%d' % ob])
            if out_f32_dram is not None:
                S.add('pool', lambda E: E.tensor_tensor(out=rof[o][0:NT, :], in0=rts[0:NT, :], in1=rtm[0:NT, :], op=ALU.add),
                      r=['rt%d' % s, 'rtmpA%d' % s, 'rtmpB%d' % s], w=['rof%d' % o])
                S.add('sp', lambda E: E.dma_start(out=out_f32_dram, in_=rof[o][0:NT, :]), r=['rof%d' % o], dma=True)
            return rob[ob], 'rob%d' % ob

        def transpose_to(robt, rkey, NT, dst_fn, dst_key):
            s = cnt['kt'] % 2
            cnt['kt'] += 1
            pk = psb[:, (6 + s) * 1024:(6 + s) * 1024 + 512].rearrange("p (i t) -> p i t", i=4)

            def emit():
                def tr(E):
                    ins = None
                    for i in range(4):
                        ins = E.transpose(out=pk[:, i, 0:NT], in_=robt[0:NT, i * 128:(i + 1) * 128], identity=ident[0:NT, 0:NT])
                    return ins
                S.add('pe', tr, r=[rkey, 'ident'], w=['pk%d' % s])
                S.add('act', lambda E: E.activation(out=dst_fn(), in_=pk[:, :, 0:NT], func=AF.Copy), r=['pk%d' % s], w=[dst_key])
            deferred.append(emit)

        def v_bank(bank, bkey, NT, vrow0, out_f32_dram, c0, samp_dst=None):
            s = cnt['v'] % 4
            cnt['v'] += 1
            if samp_dst is None:
                S.add('act', lambda E: E.activation(out=vbf[s][0:NT, :], in_=bank[0:NT, :], func=AF.Copy), r=[bkey], w=['vbf%d' % s])
                S.add('sp', lambda E: E.dma_start(out=vscr[vrow0:vrow0 + NT, c0:c0 + 512], in_=vbf[s][0:NT, :]), r=['vbf%d' % s], w=['vscr'], dma=True)
            else:
                S.add('act', lambda E: E.activation(out=samp_dst, in_=bank[0:NT, :], func=AF.Copy), r=[bkey], w=['vsamp'])
            if out_f32_dram is not None:
                s3 = cnt['v'] % 2
                S.add('act', lambda E: E.activation(out=vf[s3][0:NT, :], in_=bank[0:NT, :], func=AF.Copy), r=[bkey], w=['vf%d' % s3])
                S.add('sp', lambda E: E.dma_start(out=out_f32_dram, in_=vf[s3][0:NT, :]), r=['vf%d' % s3], dma=True)

        own_blocks = [(x_own[b * 128:(b + 1) * 128, :], 128, b * 128) for b in range(8)] + [(x_samp[:, :], 16, 1024)]
        halo_blocks = [[(x_halo[hg * 1024 + b * 128: hg * 1024 + (b + 1) * 128, :], 128, b * 128) for b in range(8)] for hg in range(2)]

        def evac_halo(hg):
            def f(cg, b, NT, tok0, bank, bkey):
                ltok = hg * 1024 + b * 128
                if cg >= 2:
                    kc = cg - 2
                    robt, rkey = rope_bank(bank, bkey, 128, hg * 8 + b)
                    transpose_to(robt, rkey, 128, lambda: KT[:, kc * 4:(kc + 1) * 4, ltok:ltok + 128], 'KT')
                else:
                    v_bank(bank, bkey, 128, ltok, None, (cg % 2) * 512)
            return f

        def evac_own(cg, b, NT, tok0, bank, bkey):
            kind = cg // 2
            tblk = 16 + b
            h4 = (cg % 2) * 4
            cc = (cg % 2) * 512
            if kind == 0:
                robt, rkey = rope_bank(bank, bkey, NT, tblk)
                transpose_to(robt, rkey, NT, lambda: QTA[:, h4:h4 + 4, tok0:tok0 + NT], 'QT')
            elif kind == 1:
                if b < 8:
                    robt, rkey = rope_bank(bank, bkey, NT, tblk, out_f32_dram=k_own[tok0:tok0 + 128, cc:cc + 512])
                    transpose_to(robt, rkey, NT, lambda: KT[:, h4:h4 + 4, 2048 + tok0:2048 + tok0 + 128], 'KT')
                else:
                    robt, rkey = rope_bank(bank, bkey, NT, tblk, out_f32_dram=k_samp[:, cc:cc + 512])
                    transpose_to(robt, rkey, NT, lambda: KTs[:, h4:h4 + 4, 0:16], 'KTs')
            else:
                if b < 8:
                    v_bank(bank, bkey, NT, 2048 + tok0, v_own[tok0:tok0 + 128, cc:cc + 512], cc)
                else:
                    v_bank(bank, bkey, NT, 0, v_samp[:, cc:cc + 512], cc, samp_dst=Vs[0:16, cc:cc + 512])

        groups = [
            dict(blocks=halo_blocks[0], cols=[C_V, C_V + 512, C_K, C_K + 512], evac=evac_halo(0)),
            dict(blocks=halo_blocks[1], cols=[C_V, C_V + 512, C_K, C_K + 512], evac=evac_halo(1)),
            dict(blocks=own_blocks, cols=[C_Q, C_Q + 512, C_K, C_K + 512, C_V, C_V + 512], evac=evac_own),
        ]
        allcg = [(gi, ci) for gi, g in enumerate(groups) for ci in range(len(g['cols']))]
        ws_next = load_w(groups[0]['cols'][0])
        g0 = groups[0]['blocks']
        g0s1, g0s2 = {}, {}
        g0n0, g0n1 = [0], [0]

        def g0_advance(k):
            while g0n0[0] < min(k + 1, len(g0)):
                g0s1[g0n0[0]], g0s2[g0n0[0]] = build_block(*g0[g0n0[0]], defer='manual3')
                g0n0[0] += 1
            while g0n1[0] < min(k, len(g0)):
                g0s1.pop(g0n1[0])()
                g0n1[0] += 1
        g0_advance(1)
        g0_advance(2)
        g0s2.pop(0)()
        S.add('sp', lambda E: E.dma_start(out=tab.rearrange("p b f -> p (b f)"), in_=tab_d), w=['tab'], dma=True)
        LA = 1
        for idx, (gi, ci) in enumerate(allcg):
            g = groups[gi]
            ws = ws_next
            if idx + 1 < len(allcg) and idx > 0:
                g2, c2 = allcg[idx + 1]
                ws_next = load_w(groups[g2]['cols'][c2])
            last_cg = (ci == len(g['cols']) - 1)
            nxt = groups[gi + 1]['blocks'] if (last_cg and gi + 1 < len(groups)) else []
            st1 = {}
            st2 = {}
            n0 = [0]
            n1 = [0]

            def stage1_upto(k):
                while n0[0] < min(k + 1, len(nxt)):
                    st1[n0[0]], st2[n0[0]] = build_block(*nxt[n0[0]], defer='manual3')
                    n0[0] += 1
                while n1[0] < min(k, len(nxt)):
                    st1.pop(n1[0])()
                    n1[0] += 1
            stage1_upto(LA)
            nb = len(g['blocks'])
            for b, (src, NT, tok0) in enumerate(g['blocks']):
                stage1_upto(b + 1 + LA)
                if idx == 0:
                    g0_advance(b + 3)
                    if b == 2:
                        ws_next = load_w(groups[allcg[1][0]]['cols'][allcg[1][1]])
                bank, bkey = nat_matmul(ws, NT, tok0)
                flush_deferred(keep=1)
                if idx == 0 and (b + 1) in g0s2:
                    g0s2.pop(b + 1)()
                if b - 1 in st2:
                    st2.pop(b - 1)()
                g['evac'](ci, b, NT, tok0, bank, bkey)
            stage1_upto(len(nxt))
            for k in sorted(st2):
                st2.pop(k)()
        flush_deferred()

        S.marks['O1'] = len(S.ops)
        hw_marks = {'H/O1': A.p}
        S.barrier()

        A.p = ph_mark
        def vtile(n):
            return A.bf(n * 256)
        Vd = [dict(d1=vtile(9).rearrange("p (m e c) -> p m e c", m=9, e=2),
                   d4=vtile(12).rearrange("p (m r e c) -> p m r e c", m=3, r=4, e=2),
                   d16a=vtile(16).rearrange("p (r e c) -> p r e c", r=16, e=2),
                   d16b=vtile(16).rearrange("p (r e c) -> p r e c", r=16, e=2)) for _ in range(2)]
        vp_all = A.bf(53 * 64)
        vp = dict(d1=vp_all[:, 0:9 * 64].rearrange("p (m c) -> p m c", m=9),
                  d4=vp_all[:, 9 * 64:21 * 64].rearrange("p (m r c) -> p m r c", m=3, r=4),
                  d16a=vp_all[:, 21 * 64:37 * 64].rearrange("p (r c) -> p r c", r=16),
                  d16b=vp_all[:, 37 * 64:53 * 64].rearrange("p (r c) -> p r c", r=16))
        PTf = [A.bf(512) for _ in range(3)]
        rD = A.f32(1024)
        XYc = A.f32(2048)
        Kc = [A.bf(9 * 128).rearrange("p (m c) -> p m c", m=9) for _ in range(2)]
        Vc = [A.bf(9 * 128).rearrange("p (m c) -> p m c", m=9) for _ in range(2)]
        KTc = [A.bf(9 * 128).rearrange("p (m c) -> p m c", m=9) for _ in range(2)]
        QsBD = A.bf(4 * 8 * 8).rearrange("p (b h q) -> p b h q", b=4, h=8)
        PTs = [A.bf(80).rearrange("p (m q) -> p m q", m=10) for _ in range(2)]
        masks = A.bf(320).rearrange("p (b m q) -> p b m q", b=4, m=10)
        rDs = [A.f32(16) for _ in range(2)]
        os_ = [A.f32(16) for _ in range(2)]
        QZ = A.bf(2 * 8 * 1024).rearrange("p (e h t) -> p e h t", e=2, h=8)

        S.add('pool', lambda E: E.dma_start(out=vp_all, in_=validp_d), w=['vp'], dma=True)
        S.add('pool', lambda E: E.dma_start(out=masks.rearrange("p b m q -> p (b m q)"), in_=masks_d), w=['masks'], dma=True)
        S.add('dve', lambda E: E.memset(QZ[64:128, 0].rearrange("p h t -> p (h t)"), 0.0), w=['QZ0b'])
        S.add('act', lambda E: E.memzero(QZ[0:64, 1].rearrange("p h t -> p (h t)")), w=['QZ1a'])
        S.add('act', lambda E: E.activation(out=QZ[0:64, 0], in_=QTA[0:64, :, 0:1024], func=AF.Copy), r=['QT'], w=['QZ0a'])
        S.add('dve', lambda E: E.tensor_copy(out=QZ[64:128, 1], in_=QTA[64:128, :, 0:1024]), r=['QT'], w=['QZ1b'])
        VKEYS = ('1', '40', '41', '42', '16a', '16b')
        for vs in range(2):
            S.add('pool', lambda E, vs=vs: E.memset(Vd[vs]['d16b'][64:128].rearrange("p r e c -> p (r e c)"), 0.0), w=['Vd%d_16b' % vs])
            for nm, key in (('d1', '1'), ('d16a', '16a'), ('d16b', '16b')):
                NP = 64 if nm == 'd16b' else 128
                S.add('dve', lambda E, vs=vs, nm=nm, NP=NP: E.tensor_copy(out=Vd[vs][nm][0:NP, :, 0, 64:128], in_=vp[nm][0:NP]), r=['vp'], w=['Vd%d_%s' % (vs, key)])
                S.add('act', lambda E, vs=vs, nm=nm, NP=NP: E.activation(out=Vd[vs][nm][0:NP, :, 1, 0:64], in_=vp[nm][0:NP], func=AF.Copy), r=['vp'], w=['Vd%d_%s' % (vs, key)])
            for m in range(3):
                S.add('dve', lambda E, vs=vs, m=m: E.tensor_copy(out=Vd[vs]['d4'][:, m, :, 0, 64:128], in_=vp['d4'][:, m]), r=['vp'], w=['Vd%d_4%d' % (vs, m)])
                S.add('act', lambda E, vs=vs, m=m: E.activation(out=Vd[vs]['d4'][:, m, :, 1, 0:64], in_=vp['d4'][:, m], func=AF.Copy), r=['vp'], w=['Vd%d_4%d' % (vs, m)])

        XY = [ps[:, 0:1024], ps[:, 1024:2048]]
        ucnt = [0]
        pending_pv = []

        def attn_unit(hp, vs, kt0, kt1, nk1, qsl, NQ, V0, V1, pvlist, vkeys):
            s = ucnt[0] % 3
            ucnt[0] += 1
            pS = ps[:, (4 + s) * 512:(4 + s) * 512 + 4 * NQ].rearrange("p (c e q) -> p c e q", c=2, e=2)
            PTv = PTf[s][:, 0:4 * NQ].rearrange("p (c e q) -> p c e q", c=2, e=2)

            def qk(E):
                E.matmul(out=pS[:, 0, :, 0:NQ], lhsT=KT[:, hp, kt0], rhs=QZ[:, :, hp, qsl], start=True, stop=True)
                if NQ == 64 and hp < 7:
                    k1 = KText[:, sl(hp * 3072 + kt1.start, 128, 16)]
                else:
                    k1 = KT[:, hp, kt1]
                nrow = 64 if (NQ == 64 and hp == 7) else nk1
                return E.matmul(out=pS[0:nrow, 1, :, 0:NQ], lhsT=k1, rhs=QZ[:, :, hp, qsl], start=True, stop=True)
            S.add('pe', qk, r=['KT', 'QZ0a', 'QZ0b', 'QZ1a', 'QZ1b'], w=['pS%d' % s])
            S.add('act', lambda E: E.activation(out=PTv[:, :, :, 0:NQ], in_=pS[:, :, :, 0:NQ], func=AF.Exp, scale=SCALE),
                  r=['pS%d' % s], w=['PT%d' % s])
            mk = maskT[:, :, 0:NQ].unsqueeze(2).to_broadcast([128, 2, 2, NQ])
            S.add('dve', lambda E: E.tensor_tensor(out=PTv[:, :, :, 0:NQ], in0=PTv[:, :, :, 0:NQ], in1=mk, op=ALU.mult),
                  r=['PT%d' % s, 'maskT'], w=['PT%d' % s])

            def pv(E):
                ins = None
                for e in range(2):
                    for c, (Vt, nk) in enumerate(((V0, 128), (V1, nk1))):
                        for (qs_, osl) in pvlist:
                            ins = E.matmul(out=XY[e][:, osl], lhsT=Vt[0:nk, e, :], rhs=PTv[0:nk, c, e, qs_], start=False, stop=False,
                                           skip_group_check=True)
                return ins
            pending_pv.append(lambda: S.add('pe', pv, r=['PT%d' % s] + ['Vd%d_%s' % (vs, k) for k in vkeys], w=['acc']))
            while len(pending_pv) > 2:
                pending_pv.pop(0)()

        S.add('dve', lambda E: E.memset(QsBD.rearrange("p b h q -> p (b h q)"), 0.0), w=['QsBD'])
        for b in range(4):
            S.add('dve', lambda E, b=b: E.tensor_copy(out=QsBD[0:64, b, :, 0:4], in_=QTA[0:64, :, 1024 + 4 * b:1028 + 4 * b]), r=['QTs'], w=['QsBD'])
            S.add('dve', lambda E, b=b: E.tensor_copy(out=QsBD[64:128, b, :, 4:8], in_=QTA[64:128, :, 1024 + 4 * b:1028 + 4 * b]), r=['QTs'], w=['QsBD'])
        pkt = psb[:, 7 * 1024:7 * 1024 + 5 * 128].rearrange("p (m c) -> p m c", m=5)
        pSs = ps[:, 7 * 512 + 320:7 * 512 + 400].rearrange("p (m q) -> p m q", m=10)
        pNs = ps[:, 7 * 512 + 400:7 * 512 + 408]
        pDs = ps[:, 7 * 512 + 416:7 * 512 + 424]
        SB = ['sb67']

        def sample_stages(it):
            b, hp = it // 8, it % 8
            s = it % 2
            cs = slice(hp * 128, hp * 128 + 128)

            def st_dma():
                for (dst, src, key) in ((Kc[s], ck, 'Kc%d' % s), (Vc[s], cv, 'Vc%d' % s)):
                    S.add('pool', lambda E, dst=dst, src=src: E.dma_start(out=dst[:, 0, :], in_=src[b, 1920:2048, cs]), w=[key + 'a'], dma=True)
                    S.add('pool', lambda E, dst=dst, src=src: E.dma_start(out=dst[:, 1:5, :], in_=src[b, 1536:2048, cs].rearrange("(i r) c -> i r c", r=4)), w=[key + 'b'], dma=True)
                    S.add('pool', lambda E, dst=dst, src=src: E.dma_start(out=dst[:, 5:9, :], in_=src[b, :, cs].rearrange("(i r) c -> i r c", r=16)[:, 0:4, :]), w=[key + 'c'], dma=True)

            def st_tr():
                for (m0, m1) in ((0, 5), (5, 9)):
                    def trs(E, m0=m0, m1=m1):
                        ins = None
                        for m in range(m0, m1):
                            ins = E.transpose(out=pkt[:, m - m0, :], in_=Kc[s][:, m, :], identity=ident)
                        return ins
                    S.add('pe', trs, r=['Kc%da' % s, 'Kc%db' % s, 'Kc%dc' % s, 'ident'] + SB, w=SB)
                    S.add('act', lambda E, m0=m0, m1=m1: E.activation(out=KTc[s][:, m0:m1, :], in_=pkt[:, 0:m1 - m0, :], func=AF.Copy), r=SB, w=['KTc%d' % s] + SB)

            def st_qk():
                def qks(E):
                    for m in range(9):
                        E.matmul(out=pSs[:, m, :], lhsT=KTc[s][:, m, :], rhs=QsBD[:, b, hp, :], start=True, stop=True)
                    return E.matmul(out=pSs[0:16, 9, :], lhsT=KTs[:, hp, 0:16], rhs=QsBD[:, b, hp, :], start=True, stop=True)
                S.add('pe', qks, r=['KTc%d' % s, 'QsBD', 'KTs'] + SB, w=SB)
                S.add('act', lambda E: E.activation(out=PTs[s][:, 0:9, :], in_=pSs[:, 0:9, :], func=AF.Exp, scale=SCALE), r=SB, w=['PTsa%d' % s] + SB)
                S.add('act', lambda E: E.activation(out=PTs[s][0:16, 9, :], in_=pSs[0:16, 9, :], func=AF.Exp, scale=SCALE), r=SB, w=['PTsb%d' % s] + SB)
                S.add('dve', lambda E: E.tensor_tensor(out=PTs[s][:, 0:9, :], in0=PTs[s][:, 0:9, :], in1=masks[:, b, 0:9, :], op=ALU.mult),
                      r=['PTsa%d' % s, 'masks'], w=['PTsa%d' % s])
                S.add('dve', lambda E: E.tensor_tensor(out=PTs[s][0:16, 9, :], in0=PTs[s][0:16, 9, :], in1=masks[0:16, b, 9, :], op=ALU.mult),
                      r=['PTsb%d' % s, 'masks'], w=['PTsb%d' % s])

            def st_pv():
                def pvs(E):
                    for m in range(9):
                        E.matmul(out=pNs, lhsT=Vc[s][:, m, :], rhs=PTs[s][:, m, :], start=(m == 0), stop=False)
                    E.matmul(out=pNs, lhsT=Vs[0:16, cs], rhs=PTs[s][0:16, 9, :], start=False, stop=True)
                    for m in range(9):
                        E.matmul(out=pDs, lhsT=ones, rhs=PTs[s][:, m, :], start=(m == 0), stop=False)
                    return E.matmul(out=pDs, lhsT=ones[0:16, :], rhs=PTs[s][0:16, 9, :], start=False, stop=True)
                S.add('pe', pvs, r=['PTsa%d' % s, 'PTsb%d' % s, 'Vc%da' % s, 'Vc%db' % s, 'Vc%dc' % s, 'vsamp', 'ones'] + SB, w=SB)
                S.add('dve', lambda E: E.reciprocal(out=rDs[s][:, 0:8], in_=pDs), r=SB, w=['rDs%d' % s] + SB)
                S.add('dve', lambda E: E.tensor_tensor(out=os_[s][:, 0:8], in0=pNs, in1=rDs[s][:, 0:8], op=ALU.mult), r=['rDs%d' % s] + SB, w=['os%d' % s] + SB)
                S.add('pool', lambda E: E.tensor_copy(out=QTA[0:64, hp, 1024 + 4 * b:1028 + 4 * b], in_=os_[s][0:64, 0:4]), r=['os%d' % s, 'QsBD'], w=['QTs'])
                S.add('pool', lambda E: E.tensor_copy(out=QTA[64:128, hp, 1024 + 4 * b:1028 + 4 * b], in_=os_[s][64:128, 4:8]), r=['os%d' % s, 'QsBD'], w=['QTs'])
            return [st_dma, st_tr, st_qk, st_pv]

        stq = []
        allst = [sample_stages(it) for it in range(32)]
        stq += [allst[0][0], allst[1][0]]
        for it in range(32):
            stq += allst[it][1:]
            if it + 2 < 32:
                stq.append(allst[it + 2][0])
        stq_pos = [0]
        unit_no = [0]

        fin_pending = [None]
        hp_unit = [0]

        def pump():
            hp_unit[0] += 1
            if hp_unit[0] >= 3 and hp_unit[0] % 2 == 1 and fin_pending[0]:
                fin_pending[0].pop(0)()
            unit_no[0] += 1
            if unit_no[0] % 2 == 0 and stq_pos[0] < len(stq):
                stq[stq_pos[0]]()
                stq_pos[0] += 1

        for hp in range(8):
            vs = hp % 2
            V = Vd[vs]
            def vsrc(rows):
                return vscr[rows, hp * 128:hp * 128 + 128]

            def vdst(t, NP=128):
                return t
            c0 = hp * 128
            for which in ('d1', 'd4', 'd16'):
                for e in range(2):
                    cse = slice(c0 + 64 * e, c0 + 64 * e + 64)
                    dc = slice(64 * e, 64 * e + 64)
                    if which == 'd1':
                        S.add('sp', lambda E, V=V, cse=cse, dc=dc, e=e: E.dma_start(out=V['d1'][:, :, e, dc], in_=vscr[1920:3072, cse].rearrange("(m p) c -> p m c", p=128)),
                              w=['Vd%d_1' % vs], dma=True)
                    elif which == 'd4':
                        for m in range(3):
                            S.add('sp', lambda E, V=V, cse=cse, dc=dc, e=e, m=m: E.dma_start(out=V['d4'][:, m, :, e, dc],
                                                                                              in_=vscr[1536 + 512 * m:1536 + 512 * (m + 1), cse].rearrange("(kk r) c -> kk r c", r=4)),
                                  w=['Vd%d_4%d' % (vs, m)], dma=True)
                    else:
                        S.add('sp', lambda E, V=V, cse=cse, dc=dc, e=e: E.dma_start(out=V['d16a'][:, :, e, dc], in_=vscr[0:2048, cse].rearrange("(kk r) c -> kk r c", r=16)),
                              w=['Vd%d_16a' % vs], dma=True)
                        S.add('sp', lambda E, V=V, cse=cse, dc=dc, e=e: E.dma_start(out=V['d16b'][0:64, :, e, dc], in_=vscr[2048:3072, cse].rearrange("(kk r) c -> kk r c", r=16)),
                              w=['Vd%d_16b' % vs], dma=True)
            S.add('dve', lambda E: E.memset(ps[:, 0:2048], 0.0), w=['acc', 'accX', 'accY'])
            hp_unit[0] = 0
            for n in range(8):
                attn_unit(hp, vs, slice(1920 + 128 * n, 2048 + 128 * n), slice(2048 + 128 * n, 2176 + 128 * n), 128,
                          slice(128 * n, 128 * n + 128), 128, V['d1'][:, n], V['d1'][:, n + 1],
                          [(sl(0, 64, 2), slice(64 * n, 64 * n + 64)), (sl(1, 64, 2), slice(512 + 64 * n, 512 + 64 * n + 64))], ('1',))
                pump()
            for r in range(4):
                for n in range(2):
                    k0 = 1536 + r + 512 * n
                    attn_unit(hp, vs, sl(k0, 128, 4), sl(k0 + 512, 128, 4), 128, sl(r + 512 * n, 128, 4), 128,
                              V['d4'][:, n, r], V['d4'][:, n + 1, r],
                              [(slice(0, 128), sl((r % 2) * 512 + r // 2 + 256 * n, 128, 2))], ('4%d' % n, '4%d' % (n + 1)))
                    pump()
            for r in range(16):
                attn_unit(hp, vs, sl(r, 128, 16), sl(2048 + r, 64, 16), 128, sl(r, 64, 16), 64,
                          V['d16a'][:, r], V['d16b'][:, r],
                          [(slice(0, 64), sl((r % 2) * 512 + r // 2, 64, 8))], ('16a', '16b'))
                pump()
            while pending_pv:
                pending_pv.pop(0)()
            X, Y = XY
            S.add('act', lambda E: E.activation(out=XYc[:, 0:1024], in_=X, func=AF.Copy), r=['acc'], w=['xyA', 'accX'])
            S.add('dve', lambda E: E.tensor_copy(out=XYc[:, 1024:2048], in_=Y), r=['acc'], w=['xyB', 'accY'])

            def fin_stages(hp=hp):
                def mul(e, par):
                    rows = slice(64 * e, 64 * e + 64)
                    off = 1024 * e
                    key = 'xyA' if e == 0 else 'xyB'
                    return lambda: S.add('pool', lambda E: E.tensor_tensor(out=QTA[rows, hp, sl(par, 512, 2)], in0=XYc[rows, off + par * 512:off + (par + 1) * 512],
                                                                           in1=rD[rows, par * 512:(par + 1) * 512], op=ALU.mult),
                                         r=[key, 'rDa' if e == 0 else 'rDb', 'QT'], w=['QT'])
                return [
                    lambda: S.add('act', lambda E: E.activation(out=rD[0:64, :], in_=XYc[64:128, 0:1024], func=AF.Ln), r=['xyA'], w=['rDa']),
                    lambda: S.add('act', lambda E: E.activation(out=rD[64:128, :], in_=XYc[0:64, 1024:2048], func=AF.Ln), r=['xyB'], w=['rDb']),
                    lambda: S.add('act', lambda E: E.activation(out=rD[0:64, :], in_=rD[0:64, :], func=AF.Exp, scale=-1.0), r=['rDa'], w=['rDa']),
                    lambda: S.add('act', lambda E: E.activation(out=rD[64:128, :], in_=rD[64:128, :], func=AF.Exp, scale=-1.0), r=['rDb'], w=['rDb']),
                    mul(0, 0), mul(0, 1), mul(1, 0), mul(1, 1),
                ]
            fin_pending[0] = fin_stages()
        while fin_pending[0]:
            fin_pending[0].pop(0)()
        while stq_pos[0] < len(stq):
            stq[stq_pos[0]]()
            stq_pos[0] += 1

        S.marks['A'] = len(S.ops)
        hw_marks['A'] = A.p
        S.barrier()

        A.p = base_mark
        CT = A.bf(8 * 1040).rearrange("p (h t) -> p h t", h=8)
        o2_mark = A.p
        dz0 = A.p
        xst2 = [A.f32(2048) for _ in range(2)]
        xn2 = [A.bf(2048) for _ in range(3)]
        gbc2 = A.f32(2048)
        vcg = [A.f32(1024) for _ in range(2)]
        vcc = [A.f32(1024) for _ in range(2)]
        vco = [A.f32(1024) for _ in range(2)]
        lng = A.f32(1024)
        lnb = A.f32(1024)
        assert A.p - dz0 >= 16384, (A.p - dz0)
        Wo = arena_t[:, dz0:dz0 + 16384].bitcast(BF16).rearrange("p (j c) -> p j c", j=16)
        DZKEYS = ['xst0', 'xst1', 'xn0', 'xn1', 'xn2', 'gbc', 'lng', 'lnb'] + ['%s%d' % (k, i) for k in ('vcgA', 'vcgB', 'vcc', 'vco') for i in range(2)]
        dz_end = A.p
        xnT2 = A.bf(16 * 1040).rearrange("p (j t) -> p j t", j=16)
        W2 = [A.bf(16 * 512).rearrange("p (j c) -> p j c", j=16) for _ in range(2)]
        vcn = A.bf(9 * 1024).rearrange("p (b c) -> p b c", b=9)
        WmT = A.bf(8 * 128).rearrange("p (g t) -> p g t", g=8)
        WmS = A.bf(8 * 16).rearrange("p (g t) -> p g t", g=8)
        bsb = A.f32(1024).rearrange("p (g t) -> p g t", g=8)
        sil = [A.bf(512) for _ in range(2)]
        mtmp = [A.f32(128) for _ in range(2)]
        B = dict(xnT=xnT2, Wst=W2, xst=xst2, xn=xn2, gbc=gbc2)
        xnT, Wst = xnT2, W2

        S.add('sp', lambda E: E.dma_start(out=gbc2, in_=norm_g.partition_broadcast(128)), w=['gbc'], dma=True)
        S.add('pool', lambda E: E.dma_start(out=WmT.rearrange("p g t -> p (g t)"), in_=wsT), w=['WmT'], dma=True)
        S.add('pool', lambda E: E.dma_start(out=WmS[0:16].rearrange("p g t -> p (g t)"), in_=wsS), w=['WmS'], dma=True)

        wsv = [load_w(C_VC), load_w(C_VC + 512)]
        o2s2 = {}
        o2s2[0] = build_block(*own_blocks[0], defer='manual')
        o2s2[1] = build_block(*own_blocks[1], defer='manual')
        o2s2.pop(0)()
        S.add('sp', lambda E: E.dma_start(out=lng, in_=ln_g.partition_broadcast(128)), w=['lng'], dma=True)
        S.add('sp', lambda E: E.dma_start(out=lnb, in_=ln_b.partition_broadcast(128)), w=['lnb'], dma=True)
        S.add('sp', lambda E: E.dma_start(out=bsb.rearrange("p g t -> p (g t)"), in_=b_s.partition_broadcast(128)), w=['bsb'], dma=True)
        for b in range(9):
            NT = 128 if b < 8 else 16
            tok0 = b * 128
            if b + 2 < 9:
                o2s2[b + 2] = build_block(*own_blocks[b + 2], defer='manual')
            if b + 1 in o2s2:
                o2s2.pop(b + 1)()
            v = b % 2
            c0 = 48 + 8 * v
            g_, c_, o_ = vcg[v], vcc[v], vco[v]
            for h in range(2):
                bank, bkey = nat_matmul(wsv[h], NT, tok0)
                S.add('act', lambda E, bank=bank, NT=NT, h=h, g_=g_, c0=c0: E.activation(out=g_[0:NT, h * 512:(h + 1) * 512], in_=bank[0:NT, :], func=AF.Gelu_apprx_tanh,
                                                                                   accum_out=sm[0:NT, c0 + h:c0 + h + 1]),
                      r=[bkey], w=['vcg%s%d' % ('AB'[h], v), 'vsum%d%d' % (h, v)])
            S.add('dve', lambda E, NT=NT, c0=c0: E.tensor_tensor(out=sm[0:NT, c0 + 2:c0 + 3], in0=sm[0:NT, c0:c0 + 1], in1=sm[0:NT, c0 + 1:c0 + 2], op=ALU.add),
                  r=['vsum0%d' % v, 'vsum1%d' % v], w=['vmean%d' % v])
            S.add('dve', lambda E, NT=NT, c0=c0: E.tensor_scalar(out=sm[0:NT, c0 + 3:c0 + 4], in0=sm[0:NT, c0 + 2:c0 + 3], scalar1=1.0 / 1024, scalar2=None, op0=ALU.mult),
                  r=['vmean%d' % v], w=['vmean2%d' % v])
            S.add('dve', lambda E, NT=NT, c0=c0, g_=g_, c_=c_: E.tensor_scalar(out=c_[0:NT, :], in0=g_[0:NT, :], scalar1=sm[0:NT, c0 + 3:c0 + 4], scalar2=None, op0=ALU.subtract),
                  r=['vcgA%d' % v, 'vcgB%d' % v, 'vmean2%d' % v], w=['vcc%d' % v])
            S.add('dve', lambda E, NT=NT, c0=c0, c_=c_, o_=o_: E.scalar_tensor_tensor(out=o_[0:NT, :], in0=c_[0:NT, :], scalar=1.0, in1=c_[0:NT, :], op0=ALU.mult, op1=ALU.mult,
                                                                                    accum_out=sm[0:NT, c0 + 4:c0 + 5]),
                  r=['vcc%d' % v], w=['vco%d' % v, 'vss%d' % v])
            S.add('dve', lambda E, NT=NT, c0=c0: E.tensor_scalar(out=sm[0:NT, c0 + 5:c0 + 6], in0=sm[0:NT, c0 + 4:c0 + 5], scalar1=1.0 / 1024, scalar2=EPS, op0=ALU.mult, op1=ALU.add),
                  r=['vss%d' % v], w=['vvar%d' % v])
            S.add('act', lambda E, NT=NT, c0=c0: E.activation(out=sm[0:NT, c0 + 6:c0 + 7], in_=sm[0:NT, c0 + 5:c0 + 6], func=AF.Sqrt), r=['vvar%d' % v], w=['vsd%d' % v])
            S.add('dve', lambda E, NT=NT, c0=c0: E.reciprocal(out=sm[0:NT, c0 + 7:c0 + 8], in_=sm[0:NT, c0 + 6:c0 + 7]), r=['vsd%d' % v], w=['vrstd%d' % v])
            S.add('dve', lambda E, NT=NT, c0=c0, c_=c_, o_=o_: E.scalar_tensor_tensor(out=o_[0:NT, :], in0=c_[0:NT, :], scalar=sm[0:NT, c0 + 7:c0 + 8], in1=lng[0:NT, :], op0=ALU.mult, op1=ALU.mult),
                  r=['vcc%d' % v, 'vrstd%d' % v, 'lng'], w=['vco%d' % v])
            if b < 8:
                S.add('pool', lambda E, NT=NT, b=b, o_=o_: E.tensor_tensor(out=vcn[0:NT, b, :], in0=o_[0:NT, :], in1=lnb[0:NT, :], op=ALU.add), r=['vco%d' % v, 'lnb'], w=['vcn'])
            else:
                S.add('pool', lambda E, NT=NT, o_=o_: E.tensor_tensor(out=o_[0:NT, :], in0=o_[0:NT, :], in1=lnb[0:NT, :], op=ALU.add), r=['vco%d' % v, 'lnb'], w=['vco%d' % v])
                S.add('act', lambda E, NT=NT, b=b, o_=o_: E.activation(out=vcn[0:NT, b, :], in_=o_[0:NT, :], func=AF.Copy), r=['vco%d' % v], w=['vcn'])
            if b == 8:
                S.add('sp', lambda E, o_=o_: E.dma_start(out=vc_samp[:, :], in_=o_[0:16, :]), r=['vco%d' % v], dma=True)

        tcnt = [0]
        tpieces = [(cbase + half * 512, kind, half) for (cbase, kind) in ((C_U, 'u'), (C_GB, 'gb'), (C_GA, 'ga')) for half in range(2)]
        ws_next = load_w(tpieces[0][0])
        for pi, (pc0, kind, half) in enumerate(tpieces):
            if True:
                ws = ws_next
                if pi + 1 < len(tpieces):
                    ws_next = load_w(tpieces[pi + 1][0])
                if pi < 4:
                    S.add('pool', lambda E, q=pi: E.dma_start(out=Wo[:, :, q * 512:(q + 1) * 512], in_=w_out_v[:, :, q * 512:(q + 1) * 512]),
                          w=['Wo%d' % pi] + DZKEYS, dma=True)
                for cb in range(4):
                    hidx = half * 4 + cb
                    for (t0, nt) in ((0, 512), (512, 512), (1024, 16)):
                        s = tcnt[0] % 2
                        tcnt[0] += 1
                        bank = ps[:, (4 + s) * 512:(5 + s) * 512]

                        def mm(E, ws=ws, cb=cb, t0=t0, nt=nt, bank=bank):
                            ins = None
                            for j in range(16):
                                ins = E.matmul(out=bank[:, 0:nt], lhsT=Wst[ws][:, j, cb * 128:(cb + 1) * 128], rhs=xnT[:, j, t0:t0 + nt], start=(j == 0), stop=(j == 15))
                            return ins
                        S.add('pe', mm, r=['xnT_b%d%s' % (bb, ab) for bb in range(t0 // 128, (t0 + nt + 127) // 128) for ab in 'ab'] + ['W%d%s' % (ws, 'a' if cb < 2 else 'b')], w=['nat%d' % s])
                        if kind == 'u':
                            S.add('act', lambda E, bank=bank, hidx=hidx, t0=t0, nt=nt: E.activation(out=CT[:, hidx, t0:t0 + nt], in_=bank[:, 0:nt], func=AF.Gelu_apprx_tanh),
                                  r=['nat%d' % s], w=['CT'])
                        else:
                            dstT = CT if kind == 'gb' else QTA
                            dkey = 'CT' if kind == 'gb' else 'QT'
                            S.add('act', lambda E, bank=bank, s=s, nt=nt: E.activation(out=sil[s][:, 0:nt], in_=bank[:, 0:nt], func=AF.Silu), r=['nat%d' % s], w=['sil%d' % s])
                            S.add('dve', lambda E, dstT=dstT, hidx=hidx, t0=t0, nt=nt, s=s: E.tensor_tensor(out=dstT[:, hidx, t0:t0 + nt], in0=dstT[:, hidx, t0:t0 + nt],
                                                                                                               in1=sil[s][:, 0:nt], op=ALU.mult),
                                  r=['sil%d' % s, dkey], w=[dkey])

        for b in range(9):
            NT = 128 if b < 8 else 16
            tok0 = b * 128
            s2 = b % 2
            pmb = ps[:, (4 + 2 * s2) * 512:(4 + 2 * s2) * 512 + 8 * NT]
            pm3 = pmb.rearrange("p (g t) -> p g t", g=8)

            def mmc(E, b=b, NT=NT, pm3=pm3):
                ins = None
                for g in range(8):
                    if b < 8:
                        ins = E.matmul(out=pm3[:, g, :], lhsT=vcn[:, b, g * 128:(g + 1) * 128], rhs=WmT[:, g, :], start=True, stop=True)
                    else:
                        ins = E.matmul(out=pm3[:, g, :], lhsT=vcn[0:16, 8, g * 128:(g + 1) * 128], rhs=WmS[0:16, g, :], start=True, stop=True)
                return ins
            S.add('pe', mmc, r=['vcn', 'WmT', 'WmS'], w=['pmb%d' % s2] + (['nat0', 'nat1'] if s2 == 0 else ['pk0', 'pk1']))
            if b < 8:
                S.add('dve', lambda E, pm3=pm3: E.tensor_tensor(out=pm3, in0=pm3, in1=bsb, op=ALU.add), r=['pmb%d' % s2, 'bsb'], w=['pmb%d' % s2])
            else:
                pm4 = pmb.rearrange("p (g b t) -> p g b t", g=8, b=4)
                S.add('dve', lambda E, pm4=pm4: E.tensor_tensor(out=pm4, in0=pm4, in1=bsb[:, :, 0:4].unsqueeze(2).to_broadcast([128, 8, 4, 4]), op=ALU.add),
                      r=['pmb%d' % s2, 'bsb'], w=['pmb%d' % s2])
            S.add('dve', lambda E, pm3=pm3, tok0=tok0, NT=NT: E.tensor_tensor(out=CT[:, :, tok0:tok0 + NT], in0=pm3, in1=CT[:, :, tok0:tok0 + NT], op=ALU.mult),
                  r=['pmb%d' % s2, 'CT'], w=['CT'])

        S.marks['O2'] = len(S.ops)
        hw_marks['O2'] = A.p
        S.barrier()

        A.p = dz_end
        xst3 = [A.f32(2048) for _ in range(2)]
        yf = [A.f32(2048) for _ in range(2)]
        fgb = A.f32(2048)
        S.add('sp', lambda E: E.dma_start(out=fgb, in_=final_g.partition_broadcast(128)), w=['fgb'], dma=True)
        for bi, b in enumerate([8, 0, 1, 2, 3, 4, 5, 6, 7]):
            NT = 128 if b < 8 else 16
            tok0 = b * 128
            s = bi % 2
            src = x_own[tok0:tok0 + 128, :] if b < 8 else x_samp[:, :]
            dst = y_own[tok0:tok0 + 128, :] if b < 8 else y_samp[:, :]
            S.add('sp', lambda E, s=s, src=src, NT=NT: E.dma_start(out=xst3[s][0:NT, :], in_=src), w=['x3_%d' % s], dma=True)
            po = ps[:, s * 2048:(s + 1) * 2048]

            def mmo(E, NT=NT, tok0=tok0, po=po):
                ins = None
                for q in range(4):
                    for j in range(16):
                        lh = QTA[:, j, tok0:tok0 + NT] if j < 8 else CT[:, j - 8, tok0:tok0 + NT]
                        ins = E.matmul(out=po[0:NT, q * 512:(q + 1) * 512], lhsT=lh, rhs=Wo[:, j, q * 512:(q + 1) * 512], start=(j == 0), stop=(j == 15))
                return ins
            S.add('pe', mmo, r=['QT', 'CT', 'Wo0', 'Wo1', 'Wo2', 'Wo3'], w=['po%d' % s])
            S.add('dve', lambda E, s=s, NT=NT, po=po: E.tensor_tensor(out=yf[s][0:NT, :], in0=po[0:NT, :], in1=xst3[s][0:NT, :], op=ALU.add),
                  r=['po%d' % s, 'x3_%d' % s], w=['yf%d' % s])
            c = 20 + 4 * s
            S.add('act', lambda E, s=s, NT=NT, c=c: E.activation(out=xst3[s][0:NT, :], in_=yf[s][0:NT, :], func=AF.Square, accum_out=sm[0:NT, c:c + 1]),
                  r=['yf%d' % s], w=['x3_%d' % s, 'yss%d' % s])
            S.add('dve', lambda E, NT=NT, c=c: E.tensor_scalar(out=sm[0:NT, c + 1:c + 2], in0=sm[0:NT, c:c + 1], scalar1=1.0 / D, scalar2=EPS, op0=ALU.mult, op1=ALU.add),
                  r=['yss%d' % s], w=['yms%d' % s])
            S.add('act', lambda E, NT=NT, c=c: E.activation(out=sm[0:NT, c + 2:c + 3], in_=sm[0:NT, c + 1:c + 2], func=AF.Sqrt), r=['yms%d' % s], w=['ysd%d' % s])
            S.add('dve', lambda E, NT=NT, c=c: E.reciprocal(out=sm[0:NT, c + 3:c + 4], in_=sm[0:NT, c + 2:c + 3]), r=['ysd%d' % s], w=['yr%d' % s])
            S.add('dve', lambda E, s=s, NT=NT, c=c: E.scalar_tensor_tensor(out=yf[s][0:NT, :], in0=yf[s][0:NT, :], scalar=sm[0:NT, c + 3:c + 4], in1=fgb[0:NT, :],
                                                                          op0=ALU.mult, op1=ALU.mult),
                  r=['yf%d' % s, 'yr%d' % s, 'fgb'], w=['yf%d' % s])
            S.add('sp', lambda E, s=s, NT=NT, dst=dst: E.dma_start(out=dst, in_=yf[s][0:NT, :]), r=['yf%d' % s], dma=True)

        hw_marks['P'] = A.p
        if TRUNC is not None:
            tr = str(TRUNC)
            if '+' in tr:
                a, b = tr.split('+')
                n = S.marks[a] + int(b)
            else:
                n = S.marks[tr]
            print("TRUNC at", n, "of", len(S.ops), S.marks)
            S.ops = S.ops[:n]
        S.emit(nc, block, sems, dma_sems)
    return nc


def _host_constants(c):
    half = 32
    inv = np.exp(-np.log(np.float32(10000.0)) * np.arange(half, dtype=np.float32) / half).astype(np.float32)
    tab = np.zeros((128, 25, 128), np.float32)
    p = np.arange(128)
    for blk in range(25):
        if blk < 16:
            pos = 1024 * c - 2048 + 128 * blk + p
        elif blk < 24:
            pos = 1024 * c + 128 * (blk - 16) + p
        else:
            pos = 16384 + (p % 4)
        ang = (pos.astype(np.float32)[:, None] * inv[None, :]).astype(np.float32)
        cs, sn = np.cos(ang).astype(np.float32), np.sin(ang).astype(np.float32)
        tab[:, blk, 0:32] = cs
        tab[:, blk, 32:64] = cs
        tab[:, blk, 64:96] = -sn
        tab[:, blk, 96:128] = sn
    valid = np.ones(3072, np.float32)
    gpos = 1024 * c - 2048 + np.arange(3072)
    valid[gpos < 0] = 0.0
    vp = np.zeros((128, 53, 64), np.float32)
    kk = np.arange(128)
    for m in range(9):
        vp[:, m, :] = valid[1920 + 128 * m + kk][:, None]
    for m in range(3):
        for r in range(4):
            vp[:, 9 + m * 4 + r, :] = valid[1536 + 512 * m + r + 4 * kk][:, None]
    for r in range(16):
        vp[:, 21 + r, :] = valid[r + 16 * kk][:, None]
        vp[0:64, 37 + r, :] = valid[2048 + r + 16 * kk[:64]][:, None]
    return tab.reshape(128, -1), vp.reshape(128, -1)


def _static_constants():
    ident = np.eye(128, dtype=np.float32)
    ones = np.ones((128, 128), np.float32)
    i = np.arange(128)[:, None]
    q = np.arange(128)[None, :]
    maskT = np.zeros((128, 2, 128), np.float32)
    maskT[:, 0, :] = (i >= q)
    maskT[:, 1, :] = (i <= q)
    masks = np.zeros((128, 4, 10, 8), np.float32)
    t = np.arange(4)
    for b in range(4):
        for e in range(2):
            masks[:, b, 0, 4 * e:4 * e + 4] = (np.arange(128)[:, None] >= t[None, :])
            for r in range(4):
                masks[:, b, 1 + r, 4 * e + r] = 1.0
                masks[:, b, 5 + r, 4 * e + r] = 1.0
            for s_ in range(4):
                for tt in range(4):
                    masks[4 * b + s_, b, 9, 4 * e + tt] = float(s_ <= tt) + 2.0 * float(s_ == tt)
    return ident, ones, maskT.reshape(128, -1), masks.reshape(128, -1)


_CACHE = {}


def kernel(x_prompt, x_sample, cache_k, cache_v, norm_g, w_in, ln_g, ln_b, w_s, b_s, w_out, final_g):
    f = np.float32
    x_prompt = np.asarray(x_prompt, f)
    x_sample = np.asarray(x_sample, f)
    cache_k = np.asarray(cache_k, f)
    cache_v = np.asarray(cache_v, f)
    w_in2 = np.ascontiguousarray(np.asarray(w_in, f)[0])
    w_out2 = np.ascontiguousarray(np.asarray(w_out, f)[0])
    ws = np.asarray(w_s, f)[0]
    wm = np.tril(ws)
    wsT = np.ascontiguousarray(np.transpose(wm, (2, 0, 1))).reshape(128, 8 * 128)
    wsS = np.zeros((16, 8, 16), f)
    for b in range(4):
        wsS[4 * b:4 * b + 4, :, 4 * b:4 * b + 4] = np.transpose(wm[:, 0:4, 0:4], (2, 0, 1))
    wsS = wsS.reshape(16, 8 * 16)
    ident, ones, maskT, masks = _static_constants()

    if 'nc' not in _CACHE:
        _CACHE['nc'] = build_program()
    nc = _CACHE['nc']

    in_maps = []
    xp = x_prompt[0]
    for c in range(NCORES):
        tab, vp = _host_constants(c)
        xh = np.zeros((THALO, D), f)
        lo = 1024 * c - 2048
        if lo >= 0:
            xh[:] = xp[lo:lo + 2048]
        elif lo + 2048 > 0:
            xh[-(lo + 2048):] = xp[0:lo + 2048]
        in_maps.append(dict(
            x_own=np.ascontiguousarray(xp[1024 * c:1024 * (c + 1)]),
            x_halo=xh,
            x_samp=np.ascontiguousarray(x_sample[4 * c:4 * c + 4].reshape(16, D)),
            ck=np.ascontiguousarray(cache_k[0, 4 * c:4 * c + 4].reshape(4, 2048, 1024)),
            cv=np.ascontiguousarray(cache_v[0, 4 * c:4 * c + 4].reshape(4, 2048, 1024)),
            w_in=w_in2, w_out=w_out2,
            norm_g=np.ascontiguousarray(np.asarray(norm_g, f)[0]),
            final_g=np.ascontiguousarray(np.asarray(final_g, f)),
            ln_g=np.ascontiguousarray(np.asarray(ln_g, f)[0]),
            ln_b=np.ascontiguousarray(np.asarray(ln_b, f)[0]),
            b_s=np.ascontiguousarray(np.asarray(b_s, f)[0].reshape(1024)),
            wsT=wsT, wsS=wsS, tab=tab, ident=ident, ones=ones, maskT=maskT, masks=masks, validp=vp,
        ))
    res = run_bass_kernel_spmd(nc, in_maps, core_ids=list(range(NCORES)))
    R = res.results
    y_prompt = np.concatenate([R[c]["y_own"] for c in range(NCORES)], 0).reshape(1, 8192, D)
    y_sample = np.concatenate([R[c]["y_samp"].reshape(4, 4, D) for c in range(NCORES)], 0)
    k_tail = np.concatenate([R[6]["k_own"], R[7]["k_own"]], 0).reshape(1, 1, 2048, 16, 64)
    v_tail = np.concatenate([R[6]["v_own"], R[7]["v_own"]], 0).reshape(1, 1, 2048, 16, 64)
    k_new = np.concatenate([R[c]["k_samp"].reshape(4, 4, 16, 64) for c in range(NCORES)], 0)[None]
    v_new = np.concatenate([R[c]["v_samp"].reshape(4, 4, 16, 64) for c in range(NCORES)], 0)[None]
    vc_new = np.concatenate([R[c]["vc_samp"].reshape(4, 4, 1024) for c in range(NCORES)], 0)[None]
    return (y_prompt.astype(f), y_sample.astype(f), k_tail.astype(f), v_tail.astype(f),
            k_new.astype(f), v_new.astype(f), vc_new.astype(f))
```
